# Optimizing a Trainium2 kernel written in Bass

```python
import jax, jax.numpy as jnp
from jax import lax
import numpy as np

D_MODEL = 2048
BATCH = 2
SEQ = 8192
DEPTH = 2

N_MIXERS = 2
CONV_WIDTH = 3
CHUNK = 128
SG_GROUPS = 8
SG_WIDTH = D_MODEL
D_FF = 5632
RMS_EPS = 1e-5
N_A = (DEPTH + 1) // 2
N_B = DEPTH // 2

kernel_name = "hybrid_shortconv_spatialgate_convffn"


def rmsnorm(x, g):
    xf = x.astype(jnp.float32)
    inv = lax.rsqrt(jnp.mean(xf * xf, axis=-1, keepdims=True) + RMS_EPS)
    return (xf * inv).astype(x.dtype) * g


def causal_dwconv3(x, w):
    s = x.shape[1]
    xp = jnp.pad(x, ((0, 0), (CONV_WIDTH - 1, 0), (0, 0)))
    return xp[:, :s] * w[0] + xp[:, 1:s + 1] * w[1] + xp[:, 2:s + 2] * w[2]


def short_conv_mixer(h, w_in, w_conv, w_out):
    bcx = jnp.einsum('bsd,de->bse', h, w_in)
    gb, gc, xs = jnp.split(bcx, 3, axis=-1)
    y = gb * causal_dwconv3(gc * xs, w_conv)
    return jnp.einsum('bsd,de->bse', y, w_out)


def spatial_gating_mixer(h, w_in, v_norm, w_s, b_s, w_out):
    bsz, s, _ = h.shape
    z = jax.nn.gelu(jnp.einsum('bsd,de->bse', h, w_in))
    u, v = jnp.split(z, 2, axis=-1)
    v = rmsnorm(v, v_norm)
    n_chunks = s // CHUNK
    vr = v.reshape(bsz, n_chunks, CHUNK, SG_GROUPS, SG_WIDTH // SG_GROUPS)
    mask = jnp.tril(jnp.ones((CHUNK, CHUNK), dtype=w_s.dtype))
    ws = w_s * mask
    mixed = jnp.einsum('hts,bnshc->bnthc', ws, vr) + b_s.T[None, None, :, :, None]
    gate = mixed.reshape(bsz, s, SG_WIDTH)
    return jnp.einsum('bsd,de->bse', u * gate, w_out)


def conv_ffn(h, w_up, conv_w, conv_b, w_down):
    up = jnp.einsum('bsd,df->bsf', h, w_up)
    up = causal_dwconv3(up, conv_w) + conv_b
    g, a = jnp.split(up, 2, axis=-1)
    return jnp.einsum('bsf,fd->bsd', jax.nn.silu(g) * a, w_down)


def setup_inputs(seed: int = 0) -> dict:
    key = jax.random.key(seed)
    ks = jax.random.split(key, 20)
    f32 = jnp.float32
    D = D_MODEL
    def nrm(k, shape, scale):
        return jax.random.normal(k, shape, f32) * scale
    def gain(k, shape):
        return 1.0 + 0.02 * jax.random.normal(k, shape, f32)
    return {
        "x": nrm(ks[0], (BATCH, SEQ, D), 1.0),
        "a_norm": gain(ks[1], (N_A, D)),
        "a_in": nrm(ks[2], (N_A, D, 3 * D), D ** -0.5),
        "a_conv": nrm(ks[3], (N_A, CONV_WIDTH, D), CONV_WIDTH ** -0.5),
        "a_out": nrm(ks[4], (N_A, D, D), D ** -0.5),
        "b_norm": gain(ks[5], (N_B, D)),
        "b_in": nrm(ks[6], (N_B, D, 2 * SG_WIDTH), D ** -0.5),
        "b_vnorm": gain(ks[7], (N_B, SG_WIDTH)),
        "b_ws": nrm(ks[8], (N_B, SG_GROUPS, CHUNK, CHUNK), CHUNK ** -0.5),
        "b_bs": gain(ks[9], (N_B, SG_GROUPS, CHUNK)),
        "b_out": nrm(ks[10], (N_B, SG_WIDTH, D), SG_WIDTH ** -0.5),
        "f_norm": gain(ks[11], (DEPTH, D)),
        "f_up": nrm(ks[12], (DEPTH, D, 2 * D_FF), D ** -0.5),
        "f_conv_w": nrm(ks[13], (DEPTH, CONV_WIDTH, 2 * D_FF), CONV_WIDTH ** -0.5),
        "f_conv_b": nrm(ks[14], (DEPTH, 2 * D_FF), 0.01),
        "f_down": nrm(ks[15], (DEPTH, D_FF, D), D_FF ** -0.5),
        "final_norm": gain(ks[16], (D,)),
    }


def reference(x, a_norm, a_in, a_conv, a_out, b_norm, b_in, b_vnorm, b_ws, b_bs, b_out,
              f_norm, f_up, f_conv_w, f_conv_b, f_down, final_norm):
    for i in range(DEPTH):
        j = i // N_MIXERS
        if i % N_MIXERS == 0:
            h = rmsnorm(x, a_norm[j])
            x = x + short_conv_mixer(h, a_in[j], a_conv[j], a_out[j])
        else:
            h = rmsnorm(x, b_norm[j])
            x = x + spatial_gating_mixer(h, b_in[j], b_vnorm[j], b_ws[j], b_bs[j], b_out[j])
        h = rmsnorm(x, f_norm[i])
        x = x + conv_ffn(h, f_up[i], f_conv_w[i], f_conv_b[i], f_down[i])
    return rmsnorm(x, final_norm)
```

```python
from contextlib import ExitStack
import numpy as np
import concourse.bass as bass
import concourse.mybir as mybir
from concourse.bass_utils import run_bass_kernel_spmd

F32 = mybir.dt.float32
BF16 = mybir.dt.bfloat16
AF = mybir.ActivationFunctionType
ALU = mybir.AluOpType

D = 2048
DFF = 5632
NCH = 16
NFC = 44
OUT0 = 4
TOK = 2050
L = OUT0 + TOK
ST = [(0, 644), (644, 1412), (1412, 2054)]
XW = 774
NCORES = 8
EPS = 1e-5
NG = 2
GJ = NFC // NG
SLOT = 6144
NSLOT = 3
NTMP = 5
NSQ = 4
PREFETCH_H = True
FINE_FIRST = True
REORDER_FIRST = True
TMPW = 776

G_A, G_F0, G_B, G_F1, G_FIN, VN = 0, 16, 32, 48, 64, 80
AC = 96
FW = [144, 408]
FB = [672, 760]
MASK = 848
BSB = 849
NCST = BSB + 1024

ENGS = ("pe", "act", "dve", "pool", "sp")


class Sched:
    def __init__(self):
        self.streams = {e: [] for e in ENGS}
        self.cnt = {e: 0 for e in ENGS}
        self.waited = {e: {} for e in ENGS}
        self.last_w = {}
        self.readers = {}
        self.dma_cnt = {}

    def op(self, eng, fn, reads=(), writes=(), dma_sem=None):
        deps = []
        for k in reads:
            t = self.last_w.get(k)
            if t is not None:
                deps.append((t, "raw"))
        for k in writes:
            t = self.last_w.get(k)
            if t is not None:
                deps.append((t, "waw"))
            for t in self.readers.get(k, {}).values():
                deps.append((t, "war"))
        for (t, kind) in deps:
            sem, val, peng = t
            if peng == eng and eng == "pe":
                continue
            if self.waited[eng].get(sem, 0) >= val:
                continue
            self.waited[eng][sem] = val
            self.streams[eng].append(("wait", sem, val))
        if dma_sem is None:
            self.cnt[eng] += 1
            tok = ("e_" + eng, self.cnt[eng], eng)
            inc = 1
        else:
            self.dma_cnt[dma_sem] = self.dma_cnt.get(dma_sem, 0) + 16
            tok = (dma_sem, self.dma_cnt[dma_sem], "dma")
            inc = 16
        self.streams[eng].append(("op", fn, tok[0], inc))
        for k in writes:
            self.last_w[k] = tok
            self.readers[k] = {}
        for k in reads:
            self.readers.setdefault(k, {})[tok[0]] = tok
        return tok


def pages(key, c0, c1):
    return [(key, p) for p in range(c0 // 128, (c1 - 1) // 128 + 1)]


def build_program(n_st=3, stages=("A", "F0", "B", "F1")):
    nc = bass.Bass("TRN2", target_bir_lowering=False)
    S = Sched()
    es = ExitStack()

    def dram(name, shape, kind="ExternalInput"):
        return nc.dram_tensor(name, shape, F32, kind=kind).ap()

    xh = dram("xh", [128, NCH, L + 2])
    cst_d = dram("cst", [128, NCST])
    wst_d = dram("wst", [128, 1024])
    wmask_d = dram("wmask", [128, 1024])
    a_in_d = dram("a_in", [16, 128, 6144])
    a_out_d = dram("a_out", [8, 128, 4096])
    f_up_d = dram("f_up", [2 * NFC, 128, 4096])
    f_dn_d = dram("f_dn", [2 * NG * 8, 128, 2 * GJ * 128])
    b_u_d = dram("b_u", [8, 128, 4096])
    b_v_d = dram("b_v", [8, 128, 4096])
    b_o_d = dram("b_o", [8, 128, 4096])
    out_d = dram("out", [128, NCH, TOK], kind="ExternalOutput")

    def sb(name, shape, dt):
        return es.enter_context(nc.sbuf_tensor(name, shape, dt))

    x = sb("x", [128, NCH, XW], F32)
    hT = sb("hT", [128, NCH, XW], BF16)
    ACTN = 6 * 2048 + 16 * 772
    act = sb("act", [128, ACTN], BF16)
    slots = [sb(f"slot{i}", [128, SLOT], BF16) for i in range(NSLOT)]
    tmps = [sb(f"tmp{i}", [128, TMPW], F32) for i in range(NTMP)]
    cst = sb("cst_sb", [128, NCST], F32)
    wsTm = sb("wsTm", [128, 1024], F32)
    wsr = sb("wsr", [128, 6, 1024], BF16)
    ones = sb("ones", [128, 128], BF16)
    sqr = [sb(f"sqr{i}", [128, 392], BF16) for i in range(NSQ)]
    ss = sb("ss", [128, 48], F32)
    ssum = sb("ssum", [128, 8], F32)
    rv = sb("rv", [128, 8], F32)
    carry = [sb(f"carry{i}", [128, NCH, 2], BF16) for i in range(2)]
    ps = [es.enter_context(nc.psum_tensor(f"ps{i}", [128, 512], F32)) for i in range(8)]

    sem_names = ["e_" + e for e in ENGS] + [f"d_slot{i}" for i in range(NSLOT)] + [f"d_x{i}" for i in range(NCH)] + ["d_c"] + [f"d_o{i}" for i in range(NTMP)]
    sems = {n: es.enter_context(nc.semaphore(n)) for n in sem_names}

    rr = {"bank": 0, "tmp": 0, "slot": 0, "sqr": 0}

    reserved = set()

    def nbank():
        while True:
            b = rr["bank"]
            rr["bank"] = (b + 1) % 8
            if b not in reserved:
                return b

    pinned = set()

    def ntmp():
        while True:
            t = rr["tmp"]
            rr["tmp"] = (t + 1) % NTMP
            if t not in pinned:
                return t

    def load_w(src_ap, ncols):
        s = rr["slot"]
        rr["slot"] = (s + 1) % NSLOT
        S.op("pool", lambda e, s=s: e.dma_start(out=slots[s][:, 0:ncols], in_=src_ap),
             writes=[("slot", s)], dma_sem=f"d_slot{s}")
        return s

    def mm_group(bank, ncol, pairs, reads, col0=0):
        def fn(e):
            ins = None
            n = len(pairs)
            for i, (l, r) in enumerate(pairs):
                ins = e.matmul(ps[bank][:, col0:col0 + ncol], l, r, start=(i == 0), stop=(i == n - 1))
            return ins
        return S.op("pe", fn, reads=reads, writes=[("ps", bank)])

    S.op("sp", lambda e: e.dma_start(out=cst[:, :], in_=cst_d), writes=[("cst",)], dma_sem="d_c")
    S.op("sp", lambda e: e.dma_start(out=wsTm[:, :], in_=wst_d), writes=[("wsTm",)], dma_sem="d_c")
    S.op("sp", lambda e: e.dma_start(out=tmps[0][:, 0:512], in_=wmask_d[:, 0:512]), writes=[("tmp", 0)], dma_sem="d_c")
    S.op("sp", lambda e: e.dma_start(out=tmps[1][:, 0:512], in_=wmask_d[:, 512:1024]), writes=[("tmp", 1)], dma_sem="d_c")
    for kk_ in (("cst",), ("wsTm",), ("tmp", 0), ("tmp", 1)):
        S.last_w[kk_] = ("d_c", 64, "dma")
    for hf in range(2):
        S.op("dve", lambda e, hf=hf: e.tensor_tensor(out=wsTm[:, hf * 512:(hf + 1) * 512],
                                                      in0=wsTm[:, hf * 512:(hf + 1) * 512],
                                                      in1=tmps[hf][:, 0:512], op=ALU.mult),
             reads=[("wsTm",), ("tmp", hf)], writes=[("wsTm",)])
    S.op("dve", lambda e: e.memset(ones[:, :], 1.0), writes=[("ones",)])
    for i in range(2):
        S.op("dve", lambda e, i=i: e.memset(carry[i][:, :, :], 0.0), writes=[("carry", i)])

    def cc(col):
        return cst[:, col:col + 1]

    epsb = sb("epsb", [128, 1], F32)
    S.op("dve", lambda e: e.memset(epsb[:, :], EPS), writes=[("epsb",)])

    def xk(c, c0, c1):
        return [("x", c, p) for p in range(c0 // 128, (c1 - 1) // 128 + 1)]

    def norm_begin(lo, split, hi):
        return {"pieces": [(lo, split), (split, hi)], "banks": None, "lo": lo, "hi": hi}

    def nbanks(ctx):
        if ctx["banks"] is None:
            ctx["banks"] = []
            for _ in range(2):
                bk = nbank()
                reserved.add(bk)
                ctx["banks"].append(bk)
        return ctx["banks"]

    def norm_chunk(ctx, c):
        for pi, (c0, c1) in enumerate(ctx["pieces"]):
            r = rr["sqr"]
            rr["sqr"] = (r + 1) % NSQ
            bk = nbanks(ctx)[pi]
            S.op("act", lambda e, r=r, c0=c0, c1=c1: e.activation(out=sqr[r][:, 0:c1 - c0], in_=x[:, c, c0:c1], func=AF.Square),
                 reads=xk(c, c0, c1), writes=[("sqr", r)])
            S.op("pe", lambda e, r=r, c0=c0, c1=c1, bk=bk: e.matmul(ps[bk][:, 0:c1 - c0], ones[:, :], sqr[r][:, 0:c1 - c0],
                                                                     start=(c == 0), stop=(c == NCH - 1)),
                 reads=[("sqr", r), ("ones",)], writes=[("ps", bk)])

    def norm_finish(ctx, gcol, out_mode=None, after_p0=None):
        if after_p0 is not None:
            after_p0()
        if out_mode is None:
            trs = [ntmp(), ntmp()]
        else:
            trs = [ntmp()] * 2
        for pi, (c0, c1) in enumerate(ctx["pieces"]):
            bk = nbanks(ctx)[pi]
            tr = trs[pi]
            S.op("act", lambda e, c0=c0, c1=c1, bk=bk, tr=tr: e.activation(
                out=tmps[tr][:, c0:c1], in_=ps[bk][:, 0:c1 - c0], func=AF.Sqrt, scale=1.0 / D, bias=epsb[:, 0:1]),
                reads=[("ps", bk), ("epsb",)], writes=[("tmp", tr)])
        for pi, (c0, c1) in enumerate(ctx["pieces"]):
            tr = trs[pi]
            S.op("dve", lambda e, c0=c0, c1=c1, tr=tr: e.reciprocal(out=tmps[tr][:, c0:c1], in_=tmps[tr][:, c0:c1]),
                 reads=[("tmp", tr)], writes=[("tmp", tr)])
            if out_mode is None:
                for c in range(NCH):
                    S.op("dve", lambda e, c=c, c0=c0, c1=c1, tr=tr: e.scalar_tensor_tensor(
                        out=hT[:, c, c0:c1], in0=x[:, c, c0:c1], scalar=cc(gcol + c), in1=tmps[tr][:, c0:c1],
                        op0=ALU.mult, op1=ALU.mult),
                        reads=xk(c, c0, c1) + [("tmp", tr), ("cst",)], writes=[("h", pi, c)])
        for bk in nbanks(ctx):
            reserved.discard(bk)
        return trs[0]

    def hseg(pi):
        return [("h", pi, c) for c in range(NCH)]

    def hkeys(wi):
        return hseg(0) if wi == 0 else hseg(0) + hseg(1)

    def conv_windows(k, lo_local):
        a, b = ST[k]
        n = b - lo_local
        h1 = (n + 1) // 2
        return [(lo_local, h1), (lo_local + h1, n - h1)]

    def resid_add(k, c, w0, n, bank):
        a, _ = ST[k]
        c0 = w0 - a + 2
        S.op("dve", lambda e: e.tensor_tensor(out=x[:, c, c0:c0 + n], in0=x[:, c, c0:c0 + n],
                                              in1=ps[bank][:, 0:n], op=ALU.add),
             reads=xk(c, c0, c0 + n) + [("ps", bank)], writes=xk(c, c0, c0 + n))

    def proj_out(k, w_d, src_key, src_ap, wins, nk, group_idx0=0, cb=None):
        for cp in range(8):
            s = load_w(w_d[group_idx0 + cp], 2 * nk * 128)
            for ci in range(2):
                c = cp * 2 + ci
                for wi, (w0, n) in enumerate(wins):
                    bk = nbank()
                    mm_group(bk, n, [(slots[s][:, (ci * nk + kk) * 128:(ci * nk + kk + 1) * 128], src_ap(kk, w0, n))
                                     for kk in range(nk)],
                             reads=[("slot", s)] + [(src_key, kk, wi) for kk in range(nk)])
                    resid_add(k, c, w0, n, bk)
                if cb is not None and c >= 1:
                    cb(c - 1)
        if cb is not None:
            cb(NCH - 1)

    def mm_unit(bks, N, s, c0, wi, fine):
        if not fine:
            for part, bk in enumerate(bks):
                mm_group(bk, N, [(slots[s][:, (part * 16 + kk) * 128:(part * 16 + kk + 1) * 128],
                                  hT[:, kk, c0:c0 + N]) for kk in range(NCH)],
                         reads=[("slot", s)] + hkeys(wi))
            return
        for kk in range(NCH):
            for part, bk in enumerate(bks):
                S.op("pe", lambda e, part=part, bk=bk, kk=kk: e.matmul(
                    ps[bk][:, 0:N], slots[s][:, (part * 16 + kk) * 128:(part * 16 + kk + 1) * 128], hT[:, kk, c0:c0 + N],
                    start=(kk == 0), stop=(kk == NCH - 1)),
                    reads=[("slot", s)] + [("h", pi, kk) for pi in range(wi + 1)], writes=[("ps", bk)])

    def mixer_a(k, cb=None, interleave=None):
        a, b = ST[k]
        wins = conv_windows(k, a)
        yv = lambda j, t0, n: act[:, j * 772 + (t0 - a): j * 772 + (t0 - a) + n]
        for j in range(NCH):
            s = load_w(a_in_d[j], 6144)
            for wi, (w0, n) in enumerate(wins):
                N = n + 2
                c0 = w0 - a
                bks = [nbank() for _ in range(3)]
                mm_unit(bks, N, s, c0, wi, fine=(FINE_FIRST and j == 0 and wi == 0))
                bgb, bgc, bxs = bks
                t1, t2, t3 = ntmp(), ntmp(), ntmp()
                S.op("act", lambda e, t1=t1, bxs=bxs, N=N: e.activation(out=tmps[t1][:, 0:N], in_=ps[bxs][:, 0:N], func=AF.Copy),
                     reads=[("ps", bxs)], writes=[("tmp", t1)])
                S.op("dve", lambda e, t1=t1, t2=t2, bgc=bgc, N=N: e.tensor_tensor(
                    out=tmps[t2][:, 0:N], in0=ps[bgc][:, 0:N], in1=tmps[t1][:, 0:N], op=ALU.mult),
                    reads=[("ps", bgc), ("tmp", t1)], writes=[("tmp", t2)])
                S.op("act", lambda e, t2=t2, t3=t3, n=n, j=j: e.activation(
                    out=tmps[t3][:, 0:n], in_=tmps[t2][:, 2:n + 2], func=AF.Identity, scale=cc(AC + j * 3 + 2)),
                    reads=[("tmp", t2), ("cst",)], writes=[("tmp", t3)])
                for tap in (1, 0):
                    S.op("dve", lambda e, t2=t2, t3=t3, n=n, j=j, tap=tap: e.scalar_tensor_tensor(
                        out=tmps[t3][:, 0:n], in0=tmps[t2][:, tap:tap + n], scalar=cc(AC + j * 3 + tap),
                        in1=tmps[t3][:, 0:n], op0=ALU.mult, op1=ALU.add),
                        reads=[("tmp", t2), ("tmp", t3), ("cst",)], writes=[("tmp", t3)])
                S.op("dve", lambda e, t3=t3, bgb=bgb, n=n, j=j, w0=w0: e.tensor_tensor(
                    out=yv(j, w0, n), in0=ps[bgb][:, 2:n + 2], in1=tmps[t3][:, 0:n], op=ALU.mult),
                    reads=[("ps", bgb), ("tmp", t3)], writes=[("y", j, wi)])
                if interleave:
                    interleave.pop(0)()
        while interleave:
            interleave.pop(0)()
        proj_out(k, a_out_d, "y", yv, wins, NCH, cb=cb)

    def ffn(k, l, lo_local, cb=None):
        a, b = ST[k]
        wins = conv_windows(k, lo_local)
        mv = lambda jj, t0, n: act[:, jj * 772 + (t0 - a): jj * 772 + (t0 - a) + n]

        def unit(g, jj, wi, s, fine):
            j = g * GJ + jj
            w0, n = wins[wi]
            N = n + 2
            c0 = w0 - a
            bks = [nbank() for _ in range(2)]
            mm_unit(bks, N, s, c0, wi, fine)
            tg, ta = ntmp(), ntmp()
            chs = (j, NFC + j)
            for (t, bk, ch) in ((tg, bks[0], chs[0]), (ta, bks[1], chs[1])):
                S.op("act", lambda e, t=t, bk=bk, ch=ch: e.activation(
                    out=tmps[t][:, 0:n], in_=ps[bk][:, 2:n + 2], func=AF.Identity,
                    scale=cc(FW[l] + ch * 3 + 2), bias=cc(FB[l] + ch)),
                    reads=[("ps", bk), ("cst",)], writes=[("tmp", t)])
            for tap in (1, 0):
                for (t, bk, ch) in ((tg, bks[0], chs[0]), (ta, bks[1], chs[1])):
                    S.op("dve", lambda e, t=t, bk=bk, ch=ch, tap=tap: e.scalar_tensor_tensor(
                        out=tmps[t][:, 0:n], in0=ps[bk][:, tap:tap + n], scalar=cc(FW[l] + ch * 3 + tap),
                        in1=tmps[t][:, 0:n], op0=ALU.mult, op1=ALU.add),
                        reads=[("ps", bk), ("tmp", t), ("cst",)], writes=[("tmp", t)])
            S.op("act", lambda e: e.activation(out=tmps[tg][:, 0:n], in_=tmps[tg][:, 0:n], func=AF.Silu),
                 reads=[("tmp", tg)], writes=[("tmp", tg)])
            S.op("dve", lambda e: e.tensor_tensor(
                out=mv(jj, w0, n), in0=tmps[tg][:, 0:n], in1=tmps[ta][:, 0:n], op=ALU.mult),
                reads=[("tmp", tg), ("tmp", ta)], writes=[("m", jj, wi)])

        for g in range(NG):
            if g == 0 and REORDER_FIRST:
                order = [(0, 0), (1, 0), (2, 0), (0, 1), (1, 1), (2, 1)] + [(jj, wi) for jj in range(3, GJ) for wi in range(2)]
            else:
                order = [(jj, wi) for jj in range(GJ) for wi in range(2)]
            slot_of = {}
            for (jj, wi) in order:
                if jj not in slot_of:
                    slot_of[jj] = load_w(f_up_d[l * NFC + g * GJ + jj], 4096)
                unit(g, jj, wi, slot_of[jj], fine=(FINE_FIRST and g == 0 and jj == 0 and wi == 0))
            proj_out(k, f_dn_d, "m", mv, wins, GJ, group_idx0=(l * NG + g) * 8, cb=(cb if g == NG - 1 else None))

    def mixer_b(k, cb=None):
        a, b = ST[k]
        first = a + 4 if k == 0 else a
        nfull = (b - first) // 128
        tail = (b - first) % 128
        nchunk = nfull + (1 if tail else 0)
        chunks = [first + 128 * i for i in range(nchunk)]
        bw = [list(range(0, 3)), list(range(3, nfull))]
        if tail:
            tc0 = b - a + 2
            tc1 = chunks[-1] - a + 2 + 128
            S.op("dve", lambda e: e.memset(hT[:, :, tc0:tc1], 0.0), reads=[], writes=hseg(1))
        vv = lambda ci, c0, c1: act[:, ci * 2048 + c0: ci * 2048 + c1]
        UG0 = 6 * 2048
        ugv = lambda j, t0, n: act[:, UG0 + j * 772 + (t0 - a): UG0 + j * 772 + (t0 - a) + n]
        if REORDER_FIRST:
            vorder = [(q, ci) for q in (0, 1) for ci in range(min(3, nchunk))] + \
                     [(q, ci) for q in (0, 1) for ci in range(3, nchunk)] + \
                     [(q, ci) for q in range(2, 8) for ci in range(nchunk)]
        else:
            vorder = [(q, ci) for q in range(8) for ci in range(nchunk)]
        vslot = {}
        for (q, ci) in vorder:
            if q not in vslot:
                vslot[q] = load_w(b_v_d[q], 4096)
            s = vslot[q]
            t0 = chunks[ci]
            if True:
                c0 = t0 - a + 2
                bk = nbank()
                if FINE_FIRST and (q, ci) == vorder[0]:
                    for kk in range(NCH):
                        S.op("pe", lambda e, kk=kk, bk=bk, c0=c0, s=s: e.matmul(
                            ps[bk][:, 0:256], hT[:, kk, c0:c0 + 128], slots[s][:, kk * 256:(kk + 1) * 256],
                            start=(kk == 0), stop=(kk == NCH - 1)),
                            reads=[("slot", s), ("h", 0, kk)], writes=[("ps", bk)])
                else:
                    mm_group(bk, 256, [(hT[:, kk, c0:c0 + 128], slots[s][:, kk * 256:(kk + 1) * 256]) for kk in range(NCH)],
                             reads=[("slot", s)] + hseg(0 if ci < 3 else 1))
                t = ntmp()
                S.op("act", lambda e, t=t, bk=bk: e.activation(out=tmps[t][:, 0:256], in_=ps[bk][:, 0:256], func=AF.Gelu_apprx_tanh),
                     reads=[("ps", bk)], writes=[("tmp", t)])
                S.op("act", lambda e, t=t, ci=ci, q=q: e.activation(out=tmps[t][:, 256:512], in_=tmps[t][:, 0:256],
                                                                     func=AF.Square, accum_out=ss[:, ci * 8 + q:ci * 8 + q + 1]),
                     reads=[("tmp", t)], writes=[("tmp", t), ("ss", ci)])
                S.op("dve", lambda e, t=t, ci=ci, q=q: e.tensor_copy(out=vv(ci, q * 256, (q + 1) * 256), in_=tmps[t][:, 0:256]),
                     reads=[("tmp", t)], writes=[("v", ci)])
        ssv = ss[:, 0:8 * nchunk].rearrange("p (c q) -> p c q", q=8)
        S.op("dve", lambda e: e.tensor_reduce(out=ssum[:, 0:nchunk], in_=ssv, axis=mybir.AxisListType.X, op=ALU.add),
             reads=[("ss", ci) for ci in range(nchunk)], writes=[("ssum",)])
        S.op("act", lambda e: e.activation(out=rv[:, 0:nchunk], in_=ssum[:, 0:nchunk], func=AF.Sqrt, scale=1.0 / D, bias=epsb[:, 0:1]),
             reads=[("ssum",), ("epsb",)], writes=[("rv",)])
        S.op("dve", lambda e: e.reciprocal(out=rv[:, 0:nchunk], in_=rv[:, 0:nchunk]), reads=[("rv",)], writes=[("rv",)])
        for ci in range(nchunk):
            S.op("dve", lambda e, ci=ci: e.tensor_scalar(out=wsr[:, ci, :], in0=wsTm[:, :], scalar1=rv[:, ci:ci + 1],
                                                         scalar2=None, op0=ALU.mult),
                 reads=[("rv",), ("wsTm",)], writes=[("wsr", ci)])
        for jg in range(8):
            s = load_w(b_u_d[jg], 4096)
            for ji in range(2):
                j = jg * 2 + ji
                h = jg
                for wi, cis in enumerate(bw):
                    if not cis:
                        continue
                    t0 = chunks[cis[0]]
                    nf = 128 * len(cis)
                    wt = tail if wi == 1 else 0
                    n = nf + wt
                    c0 = t0 - a + 2
                    bu = nbank()
                    mm_group(bu, n, [(slots[s][:, (ji * 16 + kk) * 128:(ji * 16 + kk + 1) * 128], hT[:, kk, c0:c0 + n])
                                     for kk in range(NCH)],
                             reads=[("slot", s)] + hseg(wi))
                    bg = nbank()

                    def gfn(e, cis=cis, bg=bg, j=j, h=h, wt=wt, nf=nf):
                        ins = None
                        for i, ci in enumerate(cis):
                            ins = e.matmul(ps[bg][:, i * 128:(i + 1) * 128], vv(ci, j * 128, (j + 1) * 128),
                                           wsr[:, ci, h * 128:(h + 1) * 128], start=True, stop=True)
                        if wt:
                            ci = nchunk - 1
                            ins = e.matmul(ps[bg][:, nf:nf + wt], vv(ci, j * 128, (j + 1) * 128),
                                           wsr[:, ci, h * 128:h * 128 + wt], start=True, stop=True)
                        return ins
                    gcis = list(cis) + ([nchunk - 1] if wt else [])
                    S.op("pe", gfn, reads=[("v", ci) for ci in gcis] + [("wsr", ci) for ci in gcis], writes=[("ps", bg)])
                    t1, t2 = ntmp(), ntmp()
                    S.op("act", lambda e, t1=t1, bu=bu, n=n: e.activation(out=tmps[t1][:, 0:n], in_=ps[bu][:, 0:n], func=AF.Gelu_apprx_tanh),
                         reads=[("ps", bu)], writes=[("tmp", t1)])
                    nci = len(cis)
                    S.op("dve", lambda e, t2=t2, bg=bg, nf=nf, j=j, h=h, nci=nci: e.scalar_tensor_tensor(
                        out=tmps[t2][:, 0:nf].rearrange("p (c t) -> p c t", t=128),
                        in0=ps[bg][:, 0:nf].rearrange("p (c t) -> p c t", t=128), scalar=cc(VN + j),
                        in1=cst[:, BSB + h * 128:BSB + (h + 1) * 128].unsqueeze(1).broadcast_to([128, nci, 128]),
                        op0=ALU.mult, op1=ALU.add),
                        reads=[("ps", bg), ("cst",)], writes=[("tmp", t2)])
                    if wt:
                        S.op("dve", lambda e, t2=t2, bg=bg, nf=nf, wt=wt, j=j, h=h: e.scalar_tensor_tensor(
                            out=tmps[t2][:, nf:nf + wt], in0=ps[bg][:, nf:nf + wt], scalar=cc(VN + j),
                            in1=cst[:, BSB + h * 128:BSB + h * 128 + wt], op0=ALU.mult, op1=ALU.add),
                            reads=[("ps", bg), ("cst",)], writes=[("tmp", t2)])
                    S.op("dve", lambda e, t1=t1, t2=t2, j=j, t0=t0, n=n: e.tensor_tensor(
                        out=ugv(j, t0, n), in0=tmps[t1][:, 0:n], in1=tmps[t2][:, 0:n], op=ALU.mult),
                        reads=[("tmp", t1), ("tmp", t2)], writes=[("ug", j, wi)])
        wins = [(chunks[cis[0]], 128 * len(cis) + (tail if wi == 1 else 0)) for wi, cis in enumerate(bw) if cis]
        proj_out(k, b_o_d, "ug", ugv, wins, NCH, cb=cb)

    XG = 16

    def load_x(k, c):
        if c % XG != XG - 1:
            return
        g0 = c - (XG - 1)
        a, b = ST[k]
        ln = b - a
        S.op("sp", lambda e: e.dma_start(out=x[:, g0:g0 + XG, 0:ln + 2], in_=xh[:, g0:g0 + XG, a:a + ln + 2]),
             reads=[], writes=[kk for cc_ in range(g0, g0 + XG) for kk in xk(cc_, 0, XW)], dma_sem=f"d_x{g0 // XG}")

    def b_geom(k):
        a, b = ST[k]
        first = a + 4 if k == 0 else a
        lo = first - a + 2
        return lo, lo + 384

    def stage_list(k):
        a, b = ST[k]
        ln = b - a
        out = []

        def conv_split(lo_local):
            w = conv_windows(k, lo_local)
            return w[1][0] - a + 2
        if "A" in stages:
            out.append(("A", 0, conv_split(a), ln + 2))
        if "F0" in stages:
            out.append(("F0", 2, conv_split(a), ln + 2))
        if "B" in stages:
            lo, sp = b_geom(k)
            out.append(("B", lo, sp, ln + 2))
        lo1 = max(a, OUT0)
        if "F1" in stages:
            out.append(("F1", OUT0 + 2 if k == 0 else 2, conv_split(lo1), ln + 2))
        out.append(("FIN", lo1 - a + 2, conv_split(lo1), ln + 2))
        return out

    GC = {"A": G_A, "F0": G_F0, "B": G_B, "F1": G_F1, "FIN": G_FIN}

    def make_prefetch(kn):
        an, bn = ST[kn]
        lnn = bn - an
        _, lo_n, sp_n, hi_n = stage_list(kn)[0]
        st = {}

        def load_chunk(cn):
            t = ntmp()
            S.op("sp", lambda e: e.dma_start(out=tmps[t][:, 0:lnn + 2], in_=xh[:, cn, an:an + lnn + 2]),
                 reads=[], writes=[("tmp", t)], dma_sem=f"d_o{t}")
            return t

        def step(c):
            if c == 0:
                st["ctx"] = norm_begin(lo_n, sp_n, hi_n)
            ctxn = st["ctx"]
            p1 = {0: (0, 1, 2), 1: (3, 4, 5), 2: (6, 7, 8), 3: (9, 10, 11), 4: (12, 13, 14), 5: (15,)}
            p2 = {6 + i: (2 * i, 2 * i + 1) for i in range(8)}
            if c in p1:
                for cn in p1[c]:
                    t = load_chunk(cn)
                    for pi, (c0, c1) in enumerate(ctxn["pieces"]):
                        r = rr["sqr"]
                        rr["sqr"] = (r + 1) % NSQ
                        bk = nbanks(ctxn)[pi]
                        S.op("act", lambda e, r=r, t=t, c0=c0, c1=c1: e.activation(
                            out=sqr[r][:, 0:c1 - c0], in_=tmps[t][:, c0:c1], func=AF.Square),
                            reads=[("tmp", t)], writes=[("sqr", r)])
                        S.op("pe", lambda e, r=r, c0=c0, c1=c1, bk=bk, cn=cn: e.matmul(
                            ps[bk][:, 0:c1 - c0], ones[:, :], sqr[r][:, 0:c1 - c0], start=(cn == 0), stop=(cn == NCH - 1)),
                            reads=[("sqr", r), ("ones",)], writes=[("ps", bk)])
            if c == 6:
                st["tr"] = norm_finish(ctxn, G_A, out_mode="final")
                pinned.add(st["tr"])
            if c in p2:
                tr = st["tr"]
                for cn in p2[c]:
                    t = load_chunk(cn)
                    for pi, (c0, c1) in enumerate(ctxn["pieces"]):
                        S.op("dve", lambda e, t=t, cn=cn, c0=c0, c1=c1, tr=tr: e.scalar_tensor_tensor(
                            out=hT[:, cn, c0:c1], in0=tmps[t][:, c0:c1], scalar=cc(G_A + cn), in1=tmps[tr][:, c0:c1],
                            op0=ALU.mult, op1=ALU.mult),
                            reads=[("tmp", t), ("tmp", tr), ("cst",)], writes=[("h", pi, cn)])
            if c == 13:
                pinned.discard(st["tr"])
        return step

    for c in range(NCH):
        load_x(0, c)
    pre_h = False
    pending = []
    for k in range(n_st):
        a, b = ST[k]
        ln = b - a
        sl = stage_list(k)
        skip_first_norm = pre_h
        pre_h = False
        interleave, pending = pending, []
        if not skip_first_norm:
            ctx = norm_begin(*sl[0][1:])
            for c in range(NCH):
                norm_chunk(ctx, c)
        for si, (name, lo, sp, hi) in enumerate(sl):
            if name == "FIN":
                break
            hook = None
            l = {"F0": 0, "F1": 1}.get(name)
            if l is not None:
                def hook(l=l, k=k):
                    S.op("dve", lambda e, l=l: e.tensor_copy(out=hT[:, :, 0:2], in_=carry[l][:, :, :]),
                         reads=[("carry", l)], writes=hseg(0))
                    if l == 1 and k == 0:
                        c0 = OUT0
                        S.op("dve", lambda e, c0=c0: e.memset(hT[:, :, c0:c0 + 2], 0.0), reads=[], writes=hseg(0))
            if not (si == 0 and skip_first_norm):
                norm_finish(ctx, GC[name], after_p0=hook)
            if l is not None and k + 1 < len(ST):
                S.op("dve", lambda e, l=l, ln=ln: e.tensor_copy(out=carry[l][:, :, :], in_=hT[:, :, ln:ln + 2]),
                     reads=hseg(1), writes=[("carry", l)])
            ctx = norm_begin(*sl[si + 1][1:])
            cb = (lambda c, ctx=ctx: norm_chunk(ctx, c))
            if PREFETCH_H and sl[si + 1][0] == "FIN" and k + 1 < n_st and stage_list(k + 1)[0][0] == "A":
                pf = make_prefetch(k + 1)
                cb = (lambda c, ctx=ctx, pf=pf: (norm_chunk(ctx, c), pf(c)))
                pre_h = True
            if name == "A":
                mixer_a(k, cb, interleave)
            elif name == "F0":
                ffn(k, 0, a, cb)
            elif name == "B":
                mixer_b(k, cb)
            else:
                ffn(k, 1, max(a, OUT0), cb)
        name, lo, sp, hi = sl[-1]
        tr = norm_finish(ctx, G_FIN, out_mode="final")
        pinned.add(tr)
        nout = hi - lo
        o0 = max(a, OUT0) - OUT0
        ops = []
        for c in range(NCH):
            def fin_op(c=c, lo=lo, hi=hi, nout=nout, tr=tr, o0=o0):
                t = ntmp()
                S.op("dve", lambda e: e.scalar_tensor_tensor(
                    out=tmps[t][:, 0:nout], in0=x[:, c, lo:hi], scalar=cc(G_FIN + c), in1=tmps[tr][:, lo:hi],
                    op0=ALU.mult, op1=ALU.mult),
                    reads=xk(c, lo, hi) + [("tmp", tr), ("cst",)], writes=[("tmp", t)])
                S.op("sp", lambda e: e.dma_start(out=out_d[:, c, o0:o0 + nout], in_=tmps[t][:, 0:nout]),
                     reads=[("tmp", t)], writes=[], dma_sem=f"d_o{t}")
            ops.append(fin_op)

        def tail_op(k=k, tr=tr):
            pinned.discard(tr)
            if k + 1 < n_st:
                for c in range(NCH):
                    load_x(k + 1, c)
        ops.append(tail_op)
        if pre_h:
            pending = ops
        else:
            for f in ops:
                f()
    assert not pending

    eng_attr = {"pe": "tensor", "act": "scalar", "dve": "vector", "pool": "gpsimd", "sp": "sync"}
    with nc.Block() as block:
        def make(engname):
            def body(e):
                for item in S.streams[engname]:
                    if item[0] == "wait":
                        e.wait_ge(sems[item[1]], item[2])
                    else:
                        ins = item[1](e)
                        ins.then_inc(sems[item[2]], item[3])
                if engname == "sp":
                    for i in range(NTMP):
                        if S.dma_cnt.get(f"d_o{i}", 0) > 0:
                            e.wait_ge(sems[f"d_o{i}"], S.dma_cnt[f"d_o{i}"])
            return body
        for en in ENGS:
            getattr(block, eng_attr[en])(make(en))
    es.close()
    return nc


def _slab(W, col0, ncols):
    K = W.shape[0]
    return W[:, col0:col0 + ncols].reshape(K // 128, 128, ncols).transpose(1, 0, 2)


def _fm(v):
    return v.reshape(-1, 128).T


def prepare_inputs(x, a_norm, a_in, a_conv, a_out, b_norm, b_in, b_vnorm, b_ws, b_bs, b_out,
                   f_norm, f_up, f_conv_w, f_conv_b, f_down, final_norm):
    f = np.float32
    shared = {}
    cst = np.zeros((128, NCST), f)
    cst[:, G_A:G_A + 16] = _fm(a_norm[0])
    cst[:, G_F0:G_F0 + 16] = _fm(f_norm[0])
    cst[:, G_B:G_B + 16] = _fm(b_norm[0])
    cst[:, G_F1:G_F1 + 16] = _fm(f_norm[1])
    cst[:, G_FIN:G_FIN + 16] = _fm(final_norm)
    cst[:, VN:VN + 16] = _fm(b_vnorm[0])
    cst[:, AC:AC + 48] = np.stack([_fm(a_conv[0, t]) for t in range(3)], axis=-1).reshape(128, 48)
    for l in range(2):
        cst[:, FW[l]:FW[l] + 264] = np.stack([_fm(f_conv_w[l, t]) for t in range(3)], axis=-1).reshape(128, 264)
        cst[:, FB[l]:FB[l] + 88] = _fm(f_conv_b[l])
    cst[:, BSB:BSB + 1024] = np.broadcast_to(b_bs[0].reshape(1, 1024), (128, 1024))
    shared["wst"] = np.ascontiguousarray(b_ws[0].transpose(2, 0, 1).reshape(128, 1024))
    s_idx = np.arange(128)[:, None, None]
    t_idx = np.arange(128)[None, None, :]
    shared["wmask"] = np.ascontiguousarray(np.broadcast_to((s_idx <= t_idx), (128, 8, 128)).astype(f).reshape(128, 1024))
    W = a_in[0]
    shared["a_in"] = np.ascontiguousarray(np.stack(
        [np.stack([_slab(W, part * D + j * 128, 128) for part in range(3)], axis=1).reshape(128, 6144) for j in range(16)]))
    def pairs16(Wm):
        return np.ascontiguousarray(np.stack(
            [np.stack([_slab(Wm, (cp * 2 + ci) * 128, 128) for ci in range(2)], axis=1).reshape(128, 4096) for cp in range(8)]))
    shared["a_out"] = pairs16(a_out[0])
    shared["b_u"] = pairs16(b_in[0][:, 0:D])
    shared["b_o"] = pairs16(b_out[0])
    shared["b_v"] = np.ascontiguousarray(np.stack([_slab(b_in[0], D + q * 256, 256).reshape(128, 4096) for q in range(8)]))
    shared["f_up"] = np.ascontiguousarray(np.stack(
        [np.stack([_slab(f_up[l], part * DFF + j * 128, 128) for part in range(2)], axis=1).reshape(128, 4096)
         for l in range(2) for j in range(NFC)]))
    fd = []
    for l in range(2):
        for g in range(NG):
            Wg = f_down[l][g * GJ * 128:(g + 1) * GJ * 128]
            for cp in range(8):
                fd.append(np.stack([_slab(Wg, (cp * 2 + ci) * 128, 128) for ci in range(2)], axis=1).reshape(128, 2 * GJ * 128))
    shared["f_dn"] = np.ascontiguousarray(np.stack(fd))
    in_maps = []
    for core in range(NCORES):
        bi, qi = core // 4, core % 4
        Sq = x.shape[1]
        xp = np.zeros((L + 2, D), f)
        lo = qi * 2048 - 6
        src_lo, src_hi = max(lo, 0), min(lo + L + 2, Sq)
        xp[src_lo - lo:src_hi - lo] = x[bi, src_lo:src_hi]
        m = dict(shared)
        m["xh"] = np.ascontiguousarray(xp.reshape(L + 2, NCH, 128).transpose(2, 1, 0))
        m["cst"] = cst
        in_maps.append(m)
    return in_maps


_NC_CACHE = {}


def kernel(**inputs):
    inputs = {k: np.asarray(v, dtype=np.float32) for k, v in inputs.items()}
    in_maps = prepare_inputs(**inputs)
    if "nc" not in _NC_CACHE:
        _NC_CACHE["nc"] = build_program()
    nc = _NC_CACHE["nc"]
    res = run_bass_kernel_spmd(nc, in_maps, core_ids=list(range(NCORES)))
    B, Sq = inputs["x"].shape[0], inputs["x"].shape[1]
    out = np.empty((B, Sq, D), np.float32)
    for core in range(NCORES):
        bi, qi = core // 4, core % 4
        o = np.asarray(res.results[core]["out"]).transpose(2, 1, 0).reshape(TOK, D)
        t0 = 0 if qi == 0 else 2
        g0, g1 = qi * 2048 + t0, min(qi * 2048 + TOK, Sq)
        out[bi, g0:g1, :] = o[t0:t0 + (g1 - g0)]
    return out
```

```python
from contextlib import ExitStack
import numpy as np
import concourse.bass as bass
import concourse.mybir as mybir
from concourse.bass_utils import run_bass_kernel_spmd

F32 = mybir.dt.float32
BF16 = mybir.dt.bfloat16
AF = mybir.ActivationFunctionType
ALU = mybir.AluOpType

D = 2048
DFF = 5632
NCH = 16
NFC = 44
OUT0 = 4
TOK = 2050
L = OUT0 + TOK
ST = [(0, 644), (644, 1412), (1412, 2054)]
XW = 774
NCORES = 8
EPS = 1e-5
NG = 2
GJ = NFC // NG
SLOT = 6144
NSLOT = 3
NTMP = 6
NSQ = 4
PREFETCH_H = True
FINE_FIRST = True
REORDER_FIRST = True
TMPW = 776

G_A, G_F0, G_B, G_F1, G_FIN, VN = 0, 16, 32, 48, 64, 80
AC = 96
FW = [144, 408]
FB = [672, 760]
MASK = 848
BSB = 849
NCST = BSB + 1024

ENGS = ("pe", "act", "dve", "pool", "sp")


class Sched:
    def __init__(self):
        self.streams = {e: [] for e in ENGS}
        self.cnt = {e: 0 for e in ENGS}
        self.waited = {e: {} for e in ENGS}
        self.last_w = {}
        self.readers = {}
        self.dma_cnt = {}

    def op(self, eng, fn, reads=(), writes=(), dma_sem=None):
        deps = []
        for k in reads:
            t = self.last_w.get(k)
            if t is not None:
                deps.append((t, "raw"))
        for k in writes:
            t = self.last_w.get(k)
            if t is not None:
                deps.append((t, "waw"))
            for t in self.readers.get(k, {}).values():
                deps.append((t, "war"))
        for (t, kind) in deps:
            sem, val, peng = t
            if peng == eng and eng == "pe":
                continue
            if self.waited[eng].get(sem, 0) >= val:
                continue
            self.waited[eng][sem] = val
            self.streams[eng].append(("wait", sem, val))
        if dma_sem is None:
            self.cnt[eng] += 1
            tok = ("e_" + eng, self.cnt[eng], eng)
            inc = 1
        else:
            self.dma_cnt[dma_sem] = self.dma_cnt.get(dma_sem, 0) + 16
            tok = (dma_sem, self.dma_cnt[dma_sem], "dma")
            inc = 16
        self.streams[eng].append(("op", fn, tok[0], inc))
        for k in writes:
            self.last_w[k] = tok
            self.readers[k] = {}
        for k in reads:
            self.readers.setdefault(k, {})[tok[0]] = tok
        return tok


def pages(key, c0, c1):
    return [(key, p) for p in range(c0 // 128, (c1 - 1) // 128 + 1)]


def build_program(n_st=3, stages=("A", "F0", "B", "F1")):
    nc = bass.Bass("TRN2", target_bir_lowering=False)
    S = Sched()
    es = ExitStack()

    def dram(name, shape, kind="ExternalInput"):
        return nc.dram_tensor(name, shape, F32, kind=kind).ap()

    xh = dram("xh", [128, NCH, L + 2])
    cst_d = dram("cst", [128, NCST])
    wst_d = dram("wst", [128, 1024])
    wmask_d = dram("wmask", [128, 1024])
    a_in_d = dram("a_in", [16, 128, 6144])
    a_out_d = dram("a_out", [8, 128, 4096])
    f_up_d = dram("f_up", [2 * NFC, 128, 4096])
    f_dn_d = dram("f_dn", [2 * NG * 8, 128, 2 * GJ * 128])
    b_u_d = dram("b_u", [8, 128, 4096])
    b_v_d = dram("b_v", [8, 128, 4096])
    b_o_d = dram("b_o", [8, 128, 4096])
    out_d = dram("out", [128, NCH, TOK], kind="ExternalOutput")

    def sb(name, shape, dt):
        return es.enter_context(nc.sbuf_tensor(name, shape, dt))

    x = sb("x", [128, NCH, XW], F32)
    hT = sb("hT", [128, NCH, XW], BF16)
    ACTN = 6 * 2048 + 16 * 772
    act = sb("act", [128, ACTN], BF16)
    slots = [sb(f"slot{i}", [128, SLOT], BF16) for i in range(NSLOT)]
    tmps = [sb(f"tmp{i}", [128, TMPW], F32) for i in range(NTMP)]
    cst = sb("cst_sb", [128, NCST], F32)
    wsTm = sb("wsTm", [128, 1024], F32)
    wsr = sb("wsr", [128, 6, 1024], BF16)
    ones = sb("ones", [128, 128], BF16)
    sqr = [sb(f"sqr{i}", [128, 392], BF16) for i in range(NSQ)]
    ss = sb("ss", [128, 48], F32)
    ssum = sb("ssum", [128, 8], F32)
    rv = sb("rv", [128, 8], F32)
    carry = [sb(f"carry{i}", [128, NCH, 2], BF16) for i in range(2)]
    ps = [es.enter_context(nc.psum_tensor(f"ps{i}", [128, 512], F32)) for i in range(8)]

    sem_names = ["e_" + e for e in ENGS] + [f"d_slot{i}" for i in range(NSLOT)] + [f"d_x{i}" for i in range(NCH)] + ["d_c"] + [f"d_o{i}" for i in range(NTMP)]
    sems = {n: es.enter_context(nc.semaphore(n)) for n in sem_names}

    rr = {"bank": 0, "tmp": 0, "slot": 0, "sqr": 0}

    reserved = set()

    def nbank():
        while True:
            b = rr["bank"]
            rr["bank"] = (b + 1) % 8
            if b not in reserved:
                return b

    pinned = set()

    def ntmp():
        while True:
            t = rr["tmp"]
            rr["tmp"] = (t + 1) % NTMP
            if t not in pinned:
                return t

    def load_w(src_ap, ncols):
        s = rr["slot"]
        rr["slot"] = (s + 1) % NSLOT
        S.op("pool", lambda e, s=s: e.dma_start(out=slots[s][:, 0:ncols], in_=src_ap),
             writes=[("slot", s)], dma_sem=f"d_slot{s}")
        return s

    def mm_group(bank, ncol, pairs, reads, col0=0):
        def fn(e):
            ins = None
            n = len(pairs)
            for i, (l, r) in enumerate(pairs):
                ins = e.matmul(ps[bank][:, col0:col0 + ncol], l, r, start=(i == 0), stop=(i == n - 1))
            return ins
        return S.op("pe", fn, reads=reads, writes=[("ps", bank)])

    S.op("sp", lambda e: e.dma_start(out=cst[:, :], in_=cst_d), writes=[("cst",)], dma_sem="d_c")
    S.op("sp", lambda e: e.dma_start(out=wsTm[:, :], in_=wst_d), writes=[("wsTm",)], dma_sem="d_c")
    S.op("sp", lambda e: e.dma_start(out=tmps[0][:, 0:512], in_=wmask_d[:, 0:512]), writes=[("tmp", 0)], dma_sem="d_c")
    S.op("sp", lambda e: e.dma_start(out=tmps[1][:, 0:512], in_=wmask_d[:, 512:1024]), writes=[("tmp", 1)], dma_sem="d_c")
    for kk_ in (("cst",), ("wsTm",), ("tmp", 0), ("tmp", 1)):
        S.last_w[kk_] = ("d_c", 64, "dma")
    for hf in range(2):
        S.op("dve", lambda e, hf=hf: e.tensor_tensor(out=wsTm[:, hf * 512:(hf + 1) * 512],
                                                      in0=wsTm[:, hf * 512:(hf + 1) * 512],
                                                      in1=tmps[hf][:, 0:512], op=ALU.mult),
             reads=[("wsTm",), ("tmp", hf)], writes=[("wsTm",)])
    S.op("dve", lambda e: e.memset(ones[:, :], 1.0), writes=[("ones",)])
    for i in range(2):
        S.op("dve", lambda e, i=i: e.memset(carry[i][:, :, :], 0.0), writes=[("carry", i)])

    def cc(col):
        return cst[:, col:col + 1]

    epsb = sb("epsb", [128, 1], F32)
    S.op("dve", lambda e: e.memset(epsb[:, :], EPS), writes=[("epsb",)])

    def xk(c, c0, c1):
        return [("x", c, p) for p in range(c0 // 128, (c1 - 1) // 128 + 1)]

    def norm_begin(lo, split, hi):
        return {"pieces": [(lo, split), (split, hi)], "banks": None, "lo": lo, "hi": hi}

    def nbanks(ctx):
        if ctx["banks"] is None:
            ctx["banks"] = []
            for _ in range(2):
                bk = nbank()
                reserved.add(bk)
                ctx["banks"].append(bk)
        return ctx["banks"]

    def norm_chunk(ctx, c):
        for pi, (c0, c1) in enumerate(ctx["pieces"]):
            r = rr["sqr"]
            rr["sqr"] = (r + 1) % NSQ
            bk = nbanks(ctx)[pi]
            S.op("act", lambda e, r=r, c0=c0, c1=c1: e.activation(out=sqr[r][:, 0:c1 - c0], in_=x[:, c, c0:c1], func=AF.Square),
                 reads=xk(c, c0, c1), writes=[("sqr", r)])
            S.op("pe", lambda e, r=r, c0=c0, c1=c1, bk=bk: e.matmul(ps[bk][:, 0:c1 - c0], ones[:, :], sqr[r][:, 0:c1 - c0],
                                                                     start=(c == 0), stop=(c == NCH - 1)),
                 reads=[("sqr", r), ("ones",)], writes=[("ps", bk)])

    def norm_finish(ctx, gcol, out_mode=None, after_p0=None):
        if after_p0 is not None:
            after_p0()
        if out_mode is None:
            trs = [ntmp(), ntmp()]
        else:
            trs = [ntmp()] * 2
        for pi, (c0, c1) in enumerate(ctx["pieces"]):
            bk = nbanks(ctx)[pi]
            tr = trs[pi]
            S.op("act", lambda e, c0=c0, c1=c1, bk=bk, tr=tr: e.activation(
                out=tmps[tr][:, c0:c1], in_=ps[bk][:, 0:c1 - c0], func=AF.Sqrt, scale=1.0 / D, bias=epsb[:, 0:1]),
                reads=[("ps", bk), ("epsb",)], writes=[("tmp", tr)])
        for pi, (c0, c1) in enumerate(ctx["pieces"]):
            tr = trs[pi]
            S.op("dve", lambda e, c0=c0, c1=c1, tr=tr: e.reciprocal(out=tmps[tr][:, c0:c1], in_=tmps[tr][:, c0:c1]),
                 reads=[("tmp", tr)], writes=[("tmp", tr)])
            if out_mode is None:
                for c in range(NCH):
                    S.op("dve", lambda e, c=c, c0=c0, c1=c1, tr=tr: e.scalar_tensor_tensor(
                        out=hT[:, c, c0:c1], in0=x[:, c, c0:c1], scalar=cc(gcol + c), in1=tmps[tr][:, c0:c1],
                        op0=ALU.mult, op1=ALU.mult),
                        reads=xk(c, c0, c1) + [("tmp", tr), ("cst",)], writes=[("h", pi, c)])
        for bk in nbanks(ctx):
            reserved.discard(bk)
        return trs[0]

    def hseg(pi):
        return [("h", pi, c) for c in range(NCH)]

    def hkeys(wi):
        return hseg(0) if wi == 0 else hseg(0) + hseg(1)

    def conv_windows(k, lo_local):
        a, b = ST[k]
        n = b - lo_local
        h1 = (n + 1) // 2
        return [(lo_local, h1), (lo_local + h1, n - h1)]

    def resid_add(k, c, w0, n, bank):
        a, _ = ST[k]
        c0 = w0 - a + 2
        S.op("dve", lambda e: e.tensor_tensor(out=x[:, c, c0:c0 + n], in0=x[:, c, c0:c0 + n],
                                              in1=ps[bank][:, 0:n], op=ALU.add),
             reads=xk(c, c0, c0 + n) + [("ps", bank)], writes=xk(c, c0, c0 + n))

    def proj_out(k, w_d, src_key, src_ap, wins, nk, group_idx0=0, cb=None):
        for cp in range(8):
            s = load_w(w_d[group_idx0 + cp], 2 * nk * 128)
            for ci in range(2):
                c = cp * 2 + ci
                for wi, (w0, n) in enumerate(wins):
                    bk = nbank()
                    mm_group(bk, n, [(slots[s][:, (ci * nk + kk) * 128:(ci * nk + kk + 1) * 128], src_ap(kk, w0, n))
                                     for kk in range(nk)],
                             reads=[("slot", s)] + [(src_key, kk, wi) for kk in range(nk)])
                    resid_add(k, c, w0, n, bk)
                if cb is not None and c >= 1:
                    cb(c - 1)
        if cb is not None:
            cb(NCH - 1)

    def mm_unit(bks, N, s, c0, wi, fine):
        if not fine:
            for part, bk in enumerate(bks):
                mm_group(bk, N, [(slots[s][:, (part * 16 + kk) * 128:(part * 16 + kk + 1) * 128],
                                  hT[:, kk, c0:c0 + N]) for kk in range(NCH)],
                         reads=[("slot", s)] + hkeys(wi))
            return
        for kk in range(NCH):
            for part, bk in enumerate(bks):
                S.op("pe", lambda e, part=part, bk=bk, kk=kk: e.matmul(
                    ps[bk][:, 0:N], slots[s][:, (part * 16 + kk) * 128:(part * 16 + kk + 1) * 128], hT[:, kk, c0:c0 + N],
                    start=(kk == 0), stop=(kk == NCH - 1)),
                    reads=[("slot", s)] + [("h", pi, kk) for pi in range(wi + 1)], writes=[("ps", bk)])

    def mixer_a(k, cb=None, interleave=None):
        a, b = ST[k]
        wins = conv_windows(k, a)
        yv = lambda j, t0, n: act[:, j * 772 + (t0 - a): j * 772 + (t0 - a) + n]
        for j in range(NCH):
            s = load_w(a_in_d[j], 6144)
            for wi, (w0, n) in enumerate(wins):
                N = n + 2
                c0 = w0 - a
                bks = [nbank() for _ in range(3)]
                mm_unit(bks, N, s, c0, wi, fine=(FINE_FIRST and j == 0 and wi == 0))
                bgb, bgc, bxs = bks
                t1, t2, t3 = ntmp(), ntmp(), ntmp()
                S.op("act", lambda e, t1=t1, bxs=bxs, N=N: e.activation(out=tmps[t1][:, 0:N], in_=ps[bxs][:, 0:N], func=AF.Copy),
                     reads=[("ps", bxs)], writes=[("tmp", t1)])
                S.op("dve", lambda e, t1=t1, t2=t2, bgc=bgc, N=N: e.tensor_tensor(
                    out=tmps[t2][:, 0:N], in0=ps[bgc][:, 0:N], in1=tmps[t1][:, 0:N], op=ALU.mult),
                    reads=[("ps", bgc), ("tmp", t1)], writes=[("tmp", t2)])
                S.op("act", lambda e, t2=t2, t3=t3, n=n, j=j: e.activation(
                    out=tmps[t3][:, 0:n], in_=tmps[t2][:, 2:n + 2], func=AF.Identity, scale=cc(AC + j * 3 + 2)),
                    reads=[("tmp", t2), ("cst",)], writes=[("tmp", t3)])
                for tap in (1, 0):
                    S.op("dve", lambda e, t2=t2, t3=t3, n=n, j=j, tap=tap: e.scalar_tensor_tensor(
                        out=tmps[t3][:, 0:n], in0=tmps[t2][:, tap:tap + n], scalar=cc(AC + j * 3 + tap),
                        in1=tmps[t3][:, 0:n], op0=ALU.mult, op1=ALU.add),
                        reads=[("tmp", t2), ("tmp", t3), ("cst",)], writes=[("tmp", t3)])
                S.op("dve", lambda e, t3=t3, bgb=bgb, n=n, j=j, w0=w0: e.tensor_tensor(
                    out=yv(j, w0, n), in0=ps[bgb][:, 2:n + 2], in1=tmps[t3][:, 0:n], op=ALU.mult),
                    reads=[("ps", bgb), ("tmp", t3)], writes=[("y", j, wi)])
                if interleave:
                    interleave.pop(0)()
        while interleave:
            interleave.pop(0)()
        proj_out(k, a_out_d, "y", yv, wins, NCH, cb=cb)

    def ffn(k, l, lo_local, cb=None):
        a, b = ST[k]
        wins = conv_windows(k, lo_local)
        mv = lambda jj, t0, n: act[:, jj * 772 + (t0 - a): jj * 772 + (t0 - a) + n]

        def unit(g, jj, wi, s, fine):
            j = g * GJ + jj
            w0, n = wins[wi]
            N = n + 2
            c0 = w0 - a
            bks = [nbank() for _ in range(2)]
            mm_unit(bks, N, s, c0, wi, fine)
            tg, ta = ntmp(), ntmp()
            chs = (j, NFC + j)
            for (t, bk, ch) in ((tg, bks[0], chs[0]), (ta, bks[1], chs[1])):
                S.op("act", lambda e, t=t, bk=bk, ch=ch: e.activation(
                    out=tmps[t][:, 0:n], in_=ps[bk][:, 2:n + 2], func=AF.Identity,
                    scale=cc(FW[l] + ch * 3 + 2), bias=cc(FB[l] + ch)),
                    reads=[("ps", bk), ("cst",)], writes=[("tmp", t)])
            for tap in (1, 0):
                for (t, bk, ch) in ((tg, bks[0], chs[0]), (ta, bks[1], chs[1])):
                    S.op("dve", lambda e, t=t, bk=bk, ch=ch, tap=tap: e.scalar_tensor_tensor(
                        out=tmps[t][:, 0:n], in0=ps[bk][:, tap:tap + n], scalar=cc(FW[l] + ch * 3 + tap),
                        in1=tmps[t][:, 0:n], op0=ALU.mult, op1=ALU.add),
                        reads=[("ps", bk), ("tmp", t), ("cst",)], writes=[("tmp", t)])
            S.op("act", lambda e: e.activation(out=tmps[tg][:, 0:n], in_=tmps[tg][:, 0:n], func=AF.Silu),
                 reads=[("tmp", tg)], writes=[("tmp", tg)])
            S.op("dve", lambda e: e.tensor_tensor(
                out=mv(jj, w0, n), in0=tmps[tg][:, 0:n], in1=tmps[ta][:, 0:n], op=ALU.mult),
                reads=[("tmp", tg), ("tmp", ta)], writes=[("m", jj, wi)])

        for g in range(NG):
            if g == 0 and REORDER_FIRST:
                order = [(0, 0), (1, 0), (2, 0), (0, 1), (1, 1), (2, 1)] + [(jj, wi) for jj in range(3, GJ) for wi in range(2)]
            else:
                order = [(jj, wi) for jj in range(GJ) for wi in range(2)]
            slot_of = {}
            for (jj, wi) in order:
                if jj not in slot_of:
                    slot_of[jj] = load_w(f_up_d[l * NFC + g * GJ + jj], 4096)
                unit(g, jj, wi, slot_of[jj], fine=(FINE_FIRST and g == 0 and jj == 0 and wi == 0))
            proj_out(k, f_dn_d, "m", mv, wins, GJ, group_idx0=(l * NG + g) * 8, cb=(cb if g == NG - 1 else None))

    def mixer_b(k, cb=None):
        a, b = ST[k]
        first = a + 4 if k == 0 else a
        nfull = (b - first) // 128
        tail = (b - first) % 128
        nchunk = nfull + (1 if tail else 0)
        chunks = [first + 128 * i for i in range(nchunk)]
        bw = [list(range(0, 3)), list(range(3, nfull))]
        if tail:
            tc0 = b - a + 2
            tc1 = chunks[-1] - a + 2 + 128
            S.op("dve", lambda e: e.memset(hT[:, :, tc0:tc1], 0.0), reads=[], writes=hseg(1))
        vv = lambda ci, c0, c1: act[:, ci * 2048 + c0: ci * 2048 + c1]
        UG0 = 6 * 2048
        ugv = lambda j, t0, n: act[:, UG0 + j * 772 + (t0 - a): UG0 + j * 772 + (t0 - a) + n]
        if REORDER_FIRST:
            vorder = [(q, ci) for q in (0, 1) for ci in range(min(3, nchunk))] + \
                     [(q, ci) for q in (0, 1) for ci in range(3, nchunk)] + \
                     [(q, ci) for q in range(2, 8) for ci in range(nchunk)]
        else:
            vorder = [(q, ci) for q in range(8) for ci in range(nchunk)]
        vslot = {}
        for (q, ci) in vorder:
            if q not in vslot:
                vslot[q] = load_w(b_v_d[q], 4096)
            s = vslot[q]
            t0 = chunks[ci]
            if True:
                c0 = t0 - a + 2
                bk = nbank()
                if FINE_FIRST and (q, ci) == vorder[0]:
                    for kk in range(NCH):
                        S.op("pe", lambda e, kk=kk, bk=bk, c0=c0, s=s: e.matmul(
                            ps[bk][:, 0:256], hT[:, kk, c0:c0 + 128], slots[s][:, kk * 256:(kk + 1) * 256],
                            start=(kk == 0), stop=(kk == NCH - 1)),
                            reads=[("slot", s), ("h", 0, kk)], writes=[("ps", bk)])
                else:
                    mm_group(bk, 256, [(hT[:, kk, c0:c0 + 128], slots[s][:, kk * 256:(kk + 1) * 256]) for kk in range(NCH)],
                             reads=[("slot", s)] + hseg(0 if ci < 3 else 1))
                t = ntmp()
                S.op("act", lambda e, t=t, bk=bk: e.activation(out=tmps[t][:, 0:256], in_=ps[bk][:, 0:256], func=AF.Gelu_apprx_tanh),
                     reads=[("ps", bk)], writes=[("tmp", t)])
                S.op("act", lambda e, t=t, ci=ci, q=q: e.activation(out=tmps[t][:, 256:512], in_=tmps[t][:, 0:256],
                                                                     func=AF.Square, accum_out=ss[:, ci * 8 + q:ci * 8 + q + 1]),
                     reads=[("tmp", t)], writes=[("tmp", t), ("ss", ci)])
                S.op("dve", lambda e, t=t, ci=ci, q=q: e.tensor_copy(out=vv(ci, q * 256, (q + 1) * 256), in_=tmps[t][:, 0:256]),
                     reads=[("tmp", t)], writes=[("v", ci)])
        ssv = ss[:, 0:8 * nchunk].rearrange("p (c q) -> p c q", q=8)
        S.op("dve", lambda e: e.tensor_reduce(out=ssum[:, 0:nchunk], in_=ssv, axis=mybir.AxisListType.X, op=ALU.add),
             reads=[("ss", ci) for ci in range(nchunk)], writes=[("ssum",)])
        S.op("act", lambda e: e.activation(out=rv[:, 0:nchunk], in_=ssum[:, 0:nchunk], func=AF.Sqrt, scale=1.0 / D, bias=epsb[:, 0:1]),
             reads=[("ssum",), ("epsb",)], writes=[("rv",)])
        S.op("dve", lambda e: e.reciprocal(out=rv[:, 0:nchunk], in_=rv[:, 0:nchunk]), reads=[("rv",)], writes=[("rv",)])
        for ci in range(nchunk):
            S.op("dve", lambda e, ci=ci: e.tensor_scalar(out=wsr[:, ci, :], in0=wsTm[:, :], scalar1=rv[:, ci:ci + 1],
                                                         scalar2=None, op0=ALU.mult),
                 reads=[("rv",), ("wsTm",)], writes=[("wsr", ci)])
        for jg in range(8):
            s = load_w(b_u_d[jg], 4096)
            for ji in range(2):
                j = jg * 2 + ji
                h = jg
                for wi, cis in enumerate(bw):
                    if not cis:
                        continue
                    t0 = chunks[cis[0]]
                    nf = 128 * len(cis)
                    wt = tail if wi == 1 else 0
                    n = nf + wt
                    c0 = t0 - a + 2
                    bu = nbank()
                    mm_group(bu, n, [(slots[s][:, (ji * 16 + kk) * 128:(ji * 16 + kk + 1) * 128], hT[:, kk, c0:c0 + n])
                                     for kk in range(NCH)],
                             reads=[("slot", s)] + hseg(wi))
                    bg = nbank()

                    def gfn(e, cis=cis, bg=bg, j=j, h=h, wt=wt, nf=nf):
                        ins = None
                        for i, ci in enumerate(cis):
                            ins = e.matmul(ps[bg][:, i * 128:(i + 1) * 128], vv(ci, j * 128, (j + 1) * 128),
                                           wsr[:, ci, h * 128:(h + 1) * 128], start=True, stop=True)
                        if wt:
                            ci = nchunk - 1
                            ins = e.matmul(ps[bg][:, nf:nf + wt], vv(ci, j * 128, (j + 1) * 128),
                                           wsr[:, ci, h * 128:h * 128 + wt], start=True, stop=True)
                        return ins
                    gcis = list(cis) + ([nchunk - 1] if wt else [])
                    S.op("pe", gfn, reads=[("v", ci) for ci in gcis] + [("wsr", ci) for ci in gcis], writes=[("ps", bg)])
                    t1, t2 = ntmp(), ntmp()
                    S.op("act", lambda e, t1=t1, bu=bu, n=n: e.activation(out=tmps[t1][:, 0:n], in_=ps[bu][:, 0:n], func=AF.Gelu_apprx_tanh),
                         reads=[("ps", bu)], writes=[("tmp", t1)])
                    nci = len(cis)
                    S.op("dve", lambda e, t2=t2, bg=bg, nf=nf, j=j, h=h, nci=nci: e.scalar_tensor_tensor(
                        out=tmps[t2][:, 0:nf].rearrange("p (c t) -> p c t", t=128),
                        in0=ps[bg][:, 0:nf].rearrange("p (c t) -> p c t", t=128), scalar=cc(VN + j),
                        in1=cst[:, BSB + h * 128:BSB + (h + 1) * 128].unsqueeze(1).broadcast_to([128, nci, 128]),
                        op0=ALU.mult, op1=ALU.add),
                        reads=[("ps", bg), ("cst",)], writes=[("tmp", t2)])
                    if wt:
                        S.op("dve", lambda e, t2=t2, bg=bg, nf=nf, wt=wt, j=j, h=h: e.scalar_tensor_tensor(
                            out=tmps[t2][:, nf:nf + wt], in0=ps[bg][:, nf:nf + wt], scalar=cc(VN + j),
                            in1=cst[:, BSB + h * 128:BSB + h * 128 + wt], op0=ALU.mult, op1=ALU.add),
                            reads=[("ps", bg), ("cst",)], writes=[("tmp", t2)])
                    S.op("dve", lambda e, t1=t1, t2=t2, j=j, t0=t0, n=n: e.tensor_tensor(
                        out=ugv(j, t0, n), in0=tmps[t1][:, 0:n], in1=tmps[t2][:, 0:n], op=ALU.mult),
                        reads=[("tmp", t1), ("tmp", t2)], writes=[("ug", j, wi)])
        wins = [(chunks[cis[0]], 128 * len(cis) + (tail if wi == 1 else 0)) for wi, cis in enumerate(bw) if cis]
        proj_out(k, b_o_d, "ug", ugv, wins, NCH, cb=cb)

    XG = 16

    def load_x(k, c):
        if c % XG != XG - 1:
            return
        g0 = c - (XG - 1)
        a, b = ST[k]
        ln = b - a
        S.op("sp", lambda e: e.dma_start(out=x[:, g0:g0 + XG, 0:ln + 2], in_=xh[:, g0:g0 + XG, a:a + ln + 2]),
             reads=[], writes=[kk for cc_ in range(g0, g0 + XG) for kk in xk(cc_, 0, XW)], dma_sem=f"d_x{g0 // XG}")

    def b_geom(k):
        a, b = ST[k]
        first = a + 4 if k == 0 else a
        lo = first - a + 2
        return lo, lo + 384

    def stage_list(k):
        a, b = ST[k]
        ln = b - a
        out = []

        def conv_split(lo_local):
            w = conv_windows(k, lo_local)
            return w[1][0] - a + 2
        if "A" in stages:
            out.append(("A", 0, conv_split(a), ln + 2))
        if "F0" in stages:
            out.append(("F0", 2, conv_split(a), ln + 2))
        if "B" in stages:
            lo, sp = b_geom(k)
            out.append(("B", lo, sp, ln + 2))
        lo1 = max(a, OUT0)
        if "F1" in stages:
            out.append(("F1", OUT0 + 2 if k == 0 else 2, conv_split(lo1), ln + 2))
        out.append(("FIN", lo1 - a + 2, conv_split(lo1), ln + 2))
        return out

    GC = {"A": G_A, "F0": G_F0, "B": G_B, "F1": G_F1, "FIN": G_FIN}

    def make_prefetch(kn):
        an, bn = ST[kn]
        lnn = bn - an
        _, lo_n, sp_n, hi_n = stage_list(kn)[0]
        st = {}

        def load_chunk(cn):
            t = ntmp()
            S.op("sp", lambda e: e.dma_start(out=tmps[t][:, 0:lnn + 2], in_=xh[:, cn, an:an + lnn + 2]),
                 reads=[], writes=[("tmp", t)], dma_sem=f"d_o{t}")
            return t

        def step(c):
            if c == 0:
                st["ctx"] = norm_begin(lo_n, sp_n, hi_n)
            ctxn = st["ctx"]
            p1 = {0: (0, 1, 2), 1: (3, 4, 5), 2: (6, 7, 8), 3: (9, 10, 11), 4: (12, 13, 14), 5: (15,)}
            p2 = {6 + i: (2 * i, 2 * i + 1) for i in range(8)}
            if c in p1:
                for cn in p1[c]:
                    t = load_chunk(cn)
                    for pi, (c0, c1) in enumerate(ctxn["pieces"]):
                        r = rr["sqr"]
                        rr["sqr"] = (r + 1) % NSQ
                        bk = nbanks(ctxn)[pi]
                        S.op("act", lambda e, r=r, t=t, c0=c0, c1=c1: e.activation(
                            out=sqr[r][:, 0:c1 - c0], in_=tmps[t][:, c0:c1], func=AF.Square),
                            reads=[("tmp", t)], writes=[("sqr", r)])
                        S.op("pe", lambda e, r=r, c0=c0, c1=c1, bk=bk, cn=cn: e.matmul(
                            ps[bk][:, 0:c1 - c0], ones[:, :], sqr[r][:, 0:c1 - c0], start=(cn == 0), stop=(cn == NCH - 1)),
                            reads=[("sqr", r), ("ones",)], writes=[("ps", bk)])
            if c == 6:
                st["tr"] = norm_finish(ctxn, G_A, out_mode="final")
                pinned.add(st["tr"])
            if c in p2:
                tr = st["tr"]
                for cn in p2[c]:
                    t = load_chunk(cn)
                    for pi, (c0, c1) in enumerate(ctxn["pieces"]):
                        S.op("dve", lambda e, t=t, cn=cn, c0=c0, c1=c1, tr=tr: e.scalar_tensor_tensor(
                            out=hT[:, cn, c0:c1], in0=tmps[t][:, c0:c1], scalar=cc(G_A + cn), in1=tmps[tr][:, c0:c1],
                            op0=ALU.mult, op1=ALU.mult),
                            reads=[("tmp", t), ("tmp", tr), ("cst",)], writes=[("h", pi, cn)])
            if c == 13:
                pinned.discard(st["tr"])
        return step

    for c in range(NCH):
        load_x(0, c)
    pre_h = False
    pending = []
    for k in range(n_st):
        a, b = ST[k]
        ln = b - a
        sl = stage_list(k)
        skip_first_norm = pre_h
        pre_h = False
        interleave, pending = pending, []
        if not skip_first_norm:
            ctx = norm_begin(*sl[0][1:])
            for c in range(NCH):
                norm_chunk(ctx, c)
        for si, (name, lo, sp, hi) in enumerate(sl):
            if name == "FIN":
                break
            hook = None
            l = {"F0": 0, "F1": 1}.get(name)
            if l is not None:
                def hook(l=l, k=k):
                    S.op("dve", lambda e, l=l: e.tensor_copy(out=hT[:, :, 0:2], in_=carry[l][:, :, :]),
                         reads=[("carry", l)], writes=hseg(0))
                    if l == 1 and k == 0:
                        c0 = OUT0
                        S.op("dve", lambda e, c0=c0: e.memset(hT[:, :, c0:c0 + 2], 0.0), reads=[], writes=hseg(0))
            if not (si == 0 and skip_first_norm):
                norm_finish(ctx, GC[name], after_p0=hook)
            if l is not None and k + 1 < len(ST):
                S.op("dve", lambda e, l=l, ln=ln: e.tensor_copy(out=carry[l][:, :, :], in_=hT[:, :, ln:ln + 2]),
                     reads=hseg(1), writes=[("carry", l)])
            ctx = norm_begin(*sl[si + 1][1:])
            cb = (lambda c, ctx=ctx: norm_chunk(ctx, c))
            if PREFETCH_H and sl[si + 1][0] == "FIN" and k + 1 < n_st and stage_list(k + 1)[0][0] == "A":
                pf = make_prefetch(k + 1)
                cb = (lambda c, ctx=ctx, pf=pf: (norm_chunk(ctx, c), pf(c)))
                pre_h = True
            if name == "A":
                mixer_a(k, cb, interleave)
            elif name == "F0":
                ffn(k, 0, a, cb)
            elif name == "B":
                mixer_b(k, cb)
            else:
                ffn(k, 1, max(a, OUT0), cb)
        name, lo, sp, hi = sl[-1]
        tr = norm_finish(ctx, G_FIN, out_mode="final")
        pinned.add(tr)
        nout = hi - lo
        o0 = max(a, OUT0) - OUT0
        ops = []
        for c in range(NCH):
            def fin_op(c=c, lo=lo, hi=hi, nout=nout, tr=tr, o0=o0):
                t = ntmp()
                S.op("dve", lambda e: e.scalar_tensor_tensor(
                    out=tmps[t][:, 0:nout], in0=x[:, c, lo:hi], scalar=cc(G_FIN + c), in1=tmps[tr][:, lo:hi],
                    op0=ALU.mult, op1=ALU.mult),
                    reads=xk(c, lo, hi) + [("tmp", tr), ("cst",)], writes=[("tmp", t)])
                S.op("sp", lambda e: e.dma_start(out=out_d[:, c, o0:o0 + nout], in_=tmps[t][:, 0:nout]),
                     reads=[("tmp", t)], writes=[], dma_sem=f"d_o{t}")
            ops.append(fin_op)

        def tail_op(k=k, tr=tr):
            pinned.discard(tr)
            if k + 1 < n_st:
                for c in range(NCH):
                    load_x(k + 1, c)
        ops.append(tail_op)
        if pre_h:
            pending = ops
        else:
            for f in ops:
                f()
    assert not pending

    eng_attr = {"pe": "tensor", "act": "scalar", "dve": "vector", "pool": "gpsimd", "sp": "sync"}
    with nc.Block() as block:
        def make(engname):
            def body(e):
                for item in S.streams[engname]:
                    if item[0] == "wait":
                        e.wait_ge(sems[item[1]], item[2])
                    else:
                        ins = item[1](e)
                        ins.then_inc(sems[item[2]], item[3])
                if engname == "sp":
                    for i in range(NTMP):
                        if S.dma_cnt.get(f"d_o{i}", 0) > 0:
                            e.wait_ge(sems[f"d_o{i}"], S.dma_cnt[f"d_o{i}"])
            return body
        for en in ENGS:
            getattr(block, eng_attr[en])(make(en))
    es.close()
    return nc


def _slab(W, col0, ncols):
    K = W.shape[0]
    return W[:, col0:col0 + ncols].reshape(K // 128, 128, ncols).transpose(1, 0, 2)


def _fm(v):
    return v.reshape(-1, 128).T


def prepare_inputs(x, a_norm, a_in, a_conv, a_out, b_norm, b_in, b_vnorm, b_ws, b_bs, b_out,
                   f_norm, f_up, f_conv_w, f_conv_b, f_down, final_norm):
    f = np.float32
    shared = {}
    cst = np.zeros((128, NCST), f)
    cst[:, G_A:G_A + 16] = _fm(a_norm[0])
    cst[:, G_F0:G_F0 + 16] = _fm(f_norm[0])
    cst[:, G_B:G_B + 16] = _fm(b_norm[0])
    cst[:, G_F1:G_F1 + 16] = _fm(f_norm[1])
    cst[:, G_FIN:G_FIN + 16] = _fm(final_norm)
    cst[:, VN:VN + 16] = _fm(b_vnorm[0])
    cst[:, AC:AC + 48] = np.stack([_fm(a_conv[0, t]) for t in range(3)], axis=-1).reshape(128, 48)
    for l in range(2):
        cst[:, FW[l]:FW[l] + 264] = np.stack([_fm(f_conv_w[l, t]) for t in range(3)], axis=-1).reshape(128, 264)
        cst[:, FB[l]:FB[l] + 88] = _fm(f_conv_b[l])
    cst[:, BSB:BSB + 1024] = np.broadcast_to(b_bs[0].reshape(1, 1024), (128, 1024))
    shared["wst"] = np.ascontiguousarray(b_ws[0].transpose(2, 0, 1).reshape(128, 1024))
    s_idx = np.arange(128)[:, None, None]
    t_idx = np.arange(128)[None, None, :]
    shared["wmask"] = np.ascontiguousarray(np.broadcast_to((s_idx <= t_idx), (128, 8, 128)).astype(f).reshape(128, 1024))
    W = a_in[0]
    shared["a_in"] = np.ascontiguousarray(np.stack(
        [np.stack([_slab(W, part * D + j * 128, 128) for part in range(3)], axis=1).reshape(128, 6144) for j in range(16)]))
    def pairs16(Wm):
        return np.ascontiguousarray(np.stack(
            [np.stack([_slab(Wm, (cp * 2 + ci) * 128, 128) for ci in range(2)], axis=1).reshape(128, 4096) for cp in range(8)]))
    shared["a_out"] = pairs16(a_out[0])
    shared["b_u"] = pairs16(b_in[0][:, 0:D])
    shared["b_o"] = pairs16(b_out[0])
    shared["b_v"] = np.ascontiguousarray(np.stack([_slab(b_in[0], D + q * 256, 256).reshape(128, 4096) for q in range(8)]))
    shared["f_up"] = np.ascontiguousarray(np.stack(
        [np.stack([_slab(f_up[l], part * DFF + j * 128, 128) for part in range(2)], axis=1).reshape(128, 4096)
         for l in range(2) for j in range(NFC)]))
    fd = []
    for l in range(2):
        for g in range(NG):
            Wg = f_down[l][g * GJ * 128:(g + 1) * GJ * 128]
            for cp in range(8):
                fd.append(np.stack([_slab(Wg, (cp * 2 + ci) * 128, 128) for ci in range(2)], axis=1).reshape(128, 2 * GJ * 128))
    shared["f_dn"] = np.ascontiguousarray(np.stack(fd))
    in_maps = []
    for core in range(NCORES):
        bi, qi = core // 4, core % 4
        Sq = x.shape[1]
        xp = np.zeros((L + 2, D), f)
        lo = qi * 2048 - 6
        src_lo, src_hi = max(lo, 0), min(lo + L + 2, Sq)
        xp[src_lo - lo:src_hi - lo] = x[bi, src_lo:src_hi]
        m = dict(shared)
        m["xh"] = np.ascontiguousarray(xp.reshape(L + 2, NCH, 128).transpose(2, 1, 0))
        m["cst"] = cst
        in_maps.append(m)
    return in_maps


_NC_CACHE = {}


def kernel(**inputs):
    inputs = {k: np.asarray(v, dtype=np.float32) for k, v in inputs.items()}
    in_maps = prepare_inputs(**inputs)
    if "nc" not in _NC_CACHE:
        _NC_CACHE["nc"] = build_program()
    nc = _NC_CACHE["nc"]
    res = run_bass_kernel_spmd(nc, in_maps, core_ids=list(range(NCORES)))
    B, Sq = inputs["x"].shape[0], inputs["x"].shape[1]
    out = np.empty((B, Sq, D), np.float32)
    for core in range(NCORES):
        bi, qi = core // 4, core % 4
        o = np.asarray(res.results[core]["out"]).transpose(2, 1, 0).reshape(TOK, D)
        t0 = 0 if qi == 0 else 2
        g0, g1 = qi * 2048 + t0, min(qi * 2048 + TOK, Sq)
        out[bi, g0:g1, :] = o[t0:t0 + (g1 - g0)]
    return out
```

```python
from contextlib import ExitStack
import numpy as np
import concourse.bass as bass
import concourse.mybir as mybir
from concourse.bass_utils import run_bass_kernel_spmd

F32 = mybir.dt.float32
BF16 = mybir.dt.bfloat16
AF = mybir.ActivationFunctionType
ALU = mybir.AluOpType

D = 2048
DFF = 5632
NCH = 16
NFC = 44
OUT0 = 4
TOK = 2050
L = OUT0 + TOK
ST = [(0, 644), (644, 1412), (1412, 2054)]
XW = 774
NCORES = 8
EPS = 1e-5
NG = 2
GJ = NFC // NG
SLOT = 6144
NSLOT = 3
NTMP = 6
NSQ = 4
PREFETCH_H = True
FINE_FIRST = True
REORDER_FIRST = True
TMPW = 776

G_A, G_F0, G_B, G_F1, G_FIN, VN = 0, 16, 32, 48, 64, 80
AC = 96
FW = [144, 408]
FB = [672, 760]
MASK = 848
BSB = 849
NCST = BSB + 1024

ENGS = ("pe", "act", "dve", "pool", "sp")


class Sched:
    def __init__(self):
        self.streams = {e: [] for e in ENGS}
        self.cnt = {e: 0 for e in ENGS}
        self.waited = {e: {} for e in ENGS}
        self.last_w = {}
        self.readers = {}
        self.dma_cnt = {}

    def op(self, eng, fn, reads=(), writes=(), dma_sem=None):
        deps = []
        for k in reads:
            t = self.last_w.get(k)
            if t is not None:
                deps.append((t, "raw"))
        for k in writes:
            t = self.last_w.get(k)
            if t is not None:
                deps.append((t, "waw"))
            for t in self.readers.get(k, {}).values():
                deps.append((t, "war"))
        for (t, kind) in deps:
            sem, val, peng = t
            if peng == eng and eng == "pe":
                continue
            if self.waited[eng].get(sem, 0) >= val:
                continue
            self.waited[eng][sem] = val
            self.streams[eng].append(("wait", sem, val))
        if dma_sem is None:
            self.cnt[eng] += 1
            tok = ("e_" + eng, self.cnt[eng], eng)
            inc = 1
        else:
            self.dma_cnt[dma_sem] = self.dma_cnt.get(dma_sem, 0) + 16
            tok = (dma_sem, self.dma_cnt[dma_sem], "dma")
            inc = 16
        self.streams[eng].append(("op", fn, tok[0], inc))
        for k in writes:
            self.last_w[k] = tok
            self.readers[k] = {}
        for k in reads:
            self.readers.setdefault(k, {})[tok[0]] = tok
        return tok


def pages(key, c0, c1):
    return [(key, p) for p in range(c0 // 128, (c1 - 1) // 128 + 1)]


def build_program(n_st=3, stages=("A", "F0", "B", "F1")):
    nc = bass.Bass("TRN2", target_bir_lowering=False)
    S = Sched()
    es = ExitStack()

    def dram(name, shape, kind="ExternalInput"):
        return nc.dram_tensor(name, shape, F32, kind=kind).ap()

    xh = dram("xh", [128, NCH, L + 2])
    cst_d = dram("cst", [128, NCST])
    wst_d = dram("wst", [128, 1024])
    wmask_d = dram("wmask", [128, 1024])
    a_in_d = dram("a_in", [16, 128, 6144])
    a_out_d = dram("a_out", [8, 128, 4096])
    f_up_d = dram("f_up", [2 * NFC, 128, 4096])
    f_dn_d = dram("f_dn", [2 * NG * 8, 128, 2 * GJ * 128])
    b_u_d = dram("b_u", [8, 128, 4096])
    b_v_d = dram("b_v", [8, 128, 4096])
    b_o_d = dram("b_o", [8, 128, 4096])
    out_d = dram("out", [128, NCH, TOK], kind="ExternalOutput")

    def sb(name, shape, dt):
        return es.enter_context(nc.sbuf_tensor(name, shape, dt))

    x = sb("x", [128, NCH, XW], F32)
    hT = sb("hT", [128, NCH, XW], BF16)
    ACTN = 6 * 2048 + 16 * 772
    act = sb("act", [128, ACTN], BF16)
    slots = [sb(f"slot{i}", [128, SLOT], BF16) for i in range(NSLOT)]
    tmps = [sb(f"tmp{i}", [128, TMPW], F32) for i in range(NTMP)]
    cst = sb("cst_sb", [128, NCST], F32)
    wsTm = sb("wsTm", [128, 1024], F32)
    wsr = sb("wsr", [128, 6, 1024], BF16)
    ones = sb("ones", [128, 128], BF16)
    sqr = [sb(f"sqr{i}", [128, 392], BF16) for i in range(NSQ)]
    ss = sb("ss", [128, 48], F32)
    ssum = sb("ssum", [128, 8], F32)
    rv = sb("rv", [128, 8], F32)
    carry = [sb(f"carry{i}", [128, NCH, 2], BF16) for i in range(2)]
    ps = [es.enter_context(nc.psum_tensor(f"ps{i}", [128, 512], F32)) for i in range(8)]

    sem_names = ["e_" + e for e in ENGS] + [f"d_slot{i}" for i in range(NSLOT)] + [f"d_x{i}" for i in range(NCH)] + ["d_c"] + [f"d_o{i}" for i in range(NTMP)]
    sems = {n: es.enter_context(nc.semaphore(n)) for n in sem_names}

    rr = {"bank": 0, "tmp": 0, "slot": 0, "sqr": 0}

    reserved = set()

    def nbank():
        while True:
            b = rr["bank"]
            rr["bank"] = (b + 1) % 8
            if b not in reserved:
                return b

    pinned = set()

    def ntmp():
        while True:
            t = rr["tmp"]
            rr["tmp"] = (t + 1) % NTMP
            if t not in pinned:
                return t

    def load_w(src_ap, ncols):
        s = rr["slot"]
        rr["slot"] = (s + 1) % NSLOT
        S.op("pool", lambda e, s=s: e.dma_start(out=slots[s][:, 0:ncols], in_=src_ap),
             writes=[("slot", s)], dma_sem=f"d_slot{s}")
        return s

    def mm_group(bank, ncol, pairs, reads, col0=0):
        def fn(e):
            ins = None
            n = len(pairs)
            for i, (l, r) in enumerate(pairs):
                ins = e.matmul(ps[bank][:, col0:col0 + ncol], l, r, start=(i == 0), stop=(i == n - 1))
            return ins
        return S.op("pe", fn, reads=reads, writes=[("ps", bank)])

    S.op("sp", lambda e: e.dma_start(out=cst[:, :], in_=cst_d), writes=[("cst",)], dma_sem="d_c")
    S.op("sp", lambda e: e.dma_start(out=wsTm[:, :], in_=wst_d), writes=[("wsTm",)], dma_sem="d_c")
    S.op("sp", lambda e: e.dma_start(out=tmps[0][:, 0:512], in_=wmask_d[:, 0:512]), writes=[("tmp", 0)], dma_sem="d_c")
    S.op("sp", lambda e: e.dma_start(out=tmps[1][:, 0:512], in_=wmask_d[:, 512:1024]), writes=[("tmp", 1)], dma_sem="d_c")
    for kk_ in (("cst",), ("wsTm",), ("tmp", 0), ("tmp", 1)):
        S.last_w[kk_] = ("d_c", 64, "dma")
    for hf in range(2):
        S.op("dve", lambda e, hf=hf: e.tensor_tensor(out=wsTm[:, hf * 512:(hf + 1) * 512],
                                                      in0=wsTm[:, hf * 512:(hf + 1) * 512],
                                                      in1=tmps[hf][:, 0:512], op=ALU.mult),
             reads=[("wsTm",), ("tmp", hf)], writes=[("wsTm",)])
    S.op("dve", lambda e: e.memset(ones[:, :], 1.0), writes=[("ones",)])
    for i in range(2):
        S.op("dve", lambda e, i=i: e.memset(carry[i][:, :, :], 0.0), writes=[("carry", i)])

    def cc(col):
        return cst[:, col:col + 1]

    epsb = sb("epsb", [128, 1], F32)
    S.op("dve", lambda e: e.memset(epsb[:, :], EPS), writes=[("epsb",)])

    def xk(c, c0, c1):
        return [("x", c, p) for p in range(c0 // 128, (c1 - 1) // 128 + 1)]

    def norm_begin(lo, split, hi):
        return {"pieces": [(lo, split), (split, hi)], "banks": None, "lo": lo, "hi": hi}

    def nbanks(ctx):
        if ctx["banks"] is None:
            ctx["banks"] = []
            for _ in range(2):
                bk = nbank()
                reserved.add(bk)
                ctx["banks"].append(bk)
        return ctx["banks"]

    def norm_chunk(ctx, c):
        for pi, (c0, c1) in enumerate(ctx["pieces"]):
            r = rr["sqr"]
            rr["sqr"] = (r + 1) % NSQ
            bk = nbanks(ctx)[pi]
            S.op("act", lambda e, r=r, c0=c0, c1=c1: e.activation(out=sqr[r][:, 0:c1 - c0], in_=x[:, c, c0:c1], func=AF.Square),
                 reads=xk(c, c0, c1), writes=[("sqr", r)])
            S.op("pe", lambda e, r=r, c0=c0, c1=c1, bk=bk: e.matmul(ps[bk][:, 0:c1 - c0], ones[:, :], sqr[r][:, 0:c1 - c0],
                                                                     start=(c == 0), stop=(c == NCH - 1)),
                 reads=[("sqr", r), ("ones",)], writes=[("ps", bk)])

    def norm_finish(ctx, gcol, out_mode=None, after_p0=None):
        if after_p0 is not None:
            after_p0()
        if out_mode is None:
            trs = [ntmp(), ntmp()]
        else:
            trs = [ntmp()] * 2
        for pi, (c0, c1) in enumerate(ctx["pieces"]):
            bk = nbanks(ctx)[pi]
            tr = trs[pi]
            S.op("act", lambda e, c0=c0, c1=c1, bk=bk, tr=tr: e.activation(
                out=tmps[tr][:, c0:c1], in_=ps[bk][:, 0:c1 - c0], func=AF.Sqrt, scale=1.0 / D, bias=epsb[:, 0:1]),
                reads=[("ps", bk), ("epsb",)], writes=[("tmp", tr)])
        for pi, (c0, c1) in enumerate(ctx["pieces"]):
            tr = trs[pi]
            S.op("dve", lambda e, c0=c0, c1=c1, tr=tr: e.reciprocal(out=tmps[tr][:, c0:c1], in_=tmps[tr][:, c0:c1]),
                 reads=[("tmp", tr)], writes=[("tmp", tr)])
            if out_mode is None:
                for c in range(NCH):
                    S.op("dve", lambda e, c=c, c0=c0, c1=c1, tr=tr: e.scalar_tensor_tensor(
                        out=hT[:, c, c0:c1], in0=x[:, c, c0:c1], scalar=cc(gcol + c), in1=tmps[tr][:, c0:c1],
                        op0=ALU.mult, op1=ALU.mult),
                        reads=xk(c, c0, c1) + [("tmp", tr), ("cst",)], writes=[("h", pi, c)])
        for bk in nbanks(ctx):
            reserved.discard(bk)
        return trs[0]

    def hseg(pi):
        return [("h", pi, c) for c in range(NCH)]

    def hkeys(wi):
        return hseg(0) if wi == 0 else hseg(0) + hseg(1)

    def conv_windows(k, lo_local):
        a, b = ST[k]
        n = b - lo_local
        h1 = (n + 1) // 2
        return [(lo_local, h1), (lo_local + h1, n - h1)]

    def resid_add(k, c, w0, n, bank):
        a, _ = ST[k]
        c0 = w0 - a + 2
        S.op("dve", lambda e: e.tensor_tensor(out=x[:, c, c0:c0 + n], in0=x[:, c, c0:c0 + n],
                                              in1=ps[bank][:, 0:n], op=ALU.add),
             reads=xk(c, c0, c0 + n) + [("ps", bank)], writes=xk(c, c0, c0 + n))

    def proj_out(k, w_d, src_key, src_ap, wins, nk, group_idx0=0, cb=None):
        for cp in range(8):
            s = load_w(w_d[group_idx0 + cp], 2 * nk * 128)
            for ci in range(2):
                c = cp * 2 + ci
                for wi, (w0, n) in enumerate(wins):
                    bk = nbank()
                    mm_group(bk, n, [(slots[s][:, (ci * nk + kk) * 128:(ci * nk + kk + 1) * 128], src_ap(kk, w0, n))
                                     for kk in range(nk)],
                             reads=[("slot", s)] + [(src_key, kk, wi) for kk in range(nk)])
                    resid_add(k, c, w0, n, bk)
                if cb is not None and c >= 1:
                    cb(c - 1)
        if cb is not None:
            cb(NCH - 1)

    def mm_unit(bks, N, s, c0, wi, fine):
        if not fine:
            for part, bk in enumerate(bks):
                mm_group(bk, N, [(slots[s][:, (part * 16 + kk) * 128:(part * 16 + kk + 1) * 128],
                                  hT[:, kk, c0:c0 + N]) for kk in range(NCH)],
                         reads=[("slot", s)] + hkeys(wi))
            return
        for kk in range(NCH):
            for part, bk in enumerate(bks):
                S.op("pe", lambda e, part=part, bk=bk, kk=kk: e.matmul(
                    ps[bk][:, 0:N], slots[s][:, (part * 16 + kk) * 128:(part * 16 + kk + 1) * 128], hT[:, kk, c0:c0 + N],
                    start=(kk == 0), stop=(kk == NCH - 1)),
                    reads=[("slot", s)] + [("h", pi, kk) for pi in range(wi + 1)], writes=[("ps", bk)])

    def mixer_a(k, cb=None, interleave=None):
        a, b = ST[k]
        wins = conv_windows(k, a)
        yv = lambda j, t0, n: act[:, j * 772 + (t0 - a): j * 772 + (t0 - a) + n]
        for j in range(NCH):
            s = load_w(a_in_d[j], 6144)
            for wi, (w0, n) in enumerate(wins):
                N = n + 2
                c0 = w0 - a
                bks = [nbank() for _ in range(3)]
                mm_unit(bks, N, s, c0, wi, fine=(FINE_FIRST and j == 0 and wi == 0))
                bgb, bgc, bxs = bks
                t1, t2, t3 = ntmp(), ntmp(), ntmp()
                S.op("act", lambda e, t1=t1, bxs=bxs, N=N: e.activation(out=tmps[t1][:, 0:N], in_=ps[bxs][:, 0:N], func=AF.Copy),
                     reads=[("ps", bxs)], writes=[("tmp", t1)])
                S.op("dve", lambda e, t1=t1, t2=t2, bgc=bgc, N=N: e.tensor_tensor(
                    out=tmps[t2][:, 0:N], in0=ps[bgc][:, 0:N], in1=tmps[t1][:, 0:N], op=ALU.mult),
                    reads=[("ps", bgc), ("tmp", t1)], writes=[("tmp", t2)])
                S.op("act", lambda e, t2=t2, t3=t3, n=n, j=j: e.activation(
                    out=tmps[t3][:, 0:n], in_=tmps[t2][:, 2:n + 2], func=AF.Identity, scale=cc(AC + j * 3 + 2)),
                    reads=[("tmp", t2), ("cst",)], writes=[("tmp", t3)])
                for tap in (1, 0):
                    S.op("dve", lambda e, t2=t2, t3=t3, n=n, j=j, tap=tap: e.scalar_tensor_tensor(
                        out=tmps[t3][:, 0:n], in0=tmps[t2][:, tap:tap + n], scalar=cc(AC + j * 3 + tap),
                        in1=tmps[t3][:, 0:n], op0=ALU.mult, op1=ALU.add),
                        reads=[("tmp", t2), ("tmp", t3), ("cst",)], writes=[("tmp", t3)])
                S.op("dve", lambda e, t3=t3, bgb=bgb, n=n, j=j, w0=w0: e.tensor_tensor(
                    out=yv(j, w0, n), in0=ps[bgb][:, 2:n + 2], in1=tmps[t3][:, 0:n], op=ALU.mult),
                    reads=[("ps", bgb), ("tmp", t3)], writes=[("y", j, wi)])
                if interleave:
                    interleave.pop(0)()
        while interleave:
            interleave.pop(0)()
        proj_out(k, a_out_d, "y", yv, wins, NCH, cb=cb)

    def ffn(k, l, lo_local, cb=None):
        a, b = ST[k]
        wins = conv_windows(k, lo_local)
        mv = lambda jj, t0, n: act[:, jj * 772 + (t0 - a): jj * 772 + (t0 - a) + n]

        def unit(g, jj, wi, s, fine):
            j = g * GJ + jj
            w0, n = wins[wi]
            N = n + 2
            c0 = w0 - a
            bks = [nbank() for _ in range(2)]
            mm_unit(bks, N, s, c0, wi, fine)
            tg, ta = ntmp(), ntmp()
            chs = (j, NFC + j)
            for (t, bk, ch) in ((tg, bks[0], chs[0]), (ta, bks[1], chs[1])):
                S.op("act", lambda e, t=t, bk=bk, ch=ch: e.activation(
                    out=tmps[t][:, 0:n], in_=ps[bk][:, 2:n + 2], func=AF.Identity,
                    scale=cc(FW[l] + ch * 3 + 2), bias=cc(FB[l] + ch)),
                    reads=[("ps", bk), ("cst",)], writes=[("tmp", t)])
            for tap in (1, 0):
                for (t, bk, ch) in ((tg, bks[0], chs[0]), (ta, bks[1], chs[1])):
                    S.op("dve", lambda e, t=t, bk=bk, ch=ch, tap=tap: e.scalar_tensor_tensor(
                        out=tmps[t][:, 0:n], in0=ps[bk][:, tap:tap + n], scalar=cc(FW[l] + ch * 3 + tap),
                        in1=tmps[t][:, 0:n], op0=ALU.mult, op1=ALU.add),
                        reads=[("ps", bk), ("tmp", t), ("cst",)], writes=[("tmp", t)])
            S.op("act", lambda e: e.activation(out=tmps[tg][:, 0:n], in_=tmps[tg][:, 0:n], func=AF.Silu),
                 reads=[("tmp", tg)], writes=[("tmp", tg)])
            S.op("dve", lambda e: e.tensor_tensor(
                out=mv(jj, w0, n), in0=tmps[tg][:, 0:n], in1=tmps[ta][:, 0:n], op=ALU.mult),
                reads=[("tmp", tg), ("tmp", ta)], writes=[("m", jj, wi)])

        for g in range(NG):
            if g == 0 and REORDER_FIRST:
                order = [(0, 0), (1, 0), (2, 0), (0, 1), (1, 1), (2, 1)] + [(jj, wi) for jj in range(3, GJ) for wi in range(2)]
            else:
                order = [(jj, wi) for jj in range(GJ) for wi in range(2)]
            slot_of = {}
            for (jj, wi) in order:
                if jj not in slot_of:
                    slot_of[jj] = load_w(f_up_d[l * NFC + g * GJ + jj], 4096)
                unit(g, jj, wi, slot_of[jj], fine=(FINE_FIRST and g == 0 and jj == 0 and wi == 0))
            proj_out(k, f_dn_d, "m", mv, wins, GJ, group_idx0=(l * NG + g) * 8, cb=(cb if g == NG - 1 else None))

    def mixer_b(k, cb=None):
        a, b = ST[k]
        first = a + 4 if k == 0 else a
        nfull = (b - first) // 128
        tail = (b - first) % 128
        nchunk = nfull + (1 if tail else 0)
        chunks = [first + 128 * i for i in range(nchunk)]
        bw = [list(range(0, 3)), list(range(3, nfull))]
        if tail:
            tc0 = b - a + 2
            tc1 = chunks[-1] - a + 2 + 128
            S.op("dve", lambda e: e.memset(hT[:, :, tc0:tc1], 0.0), reads=[], writes=hseg(1))
        vv = lambda ci, c0, c1: act[:, ci * 2048 + c0: ci * 2048 + c1]
        UG0 = 6 * 2048
        ugv = lambda j, t0, n: act[:, UG0 + j * 772 + (t0 - a): UG0 + j * 772 + (t0 - a) + n]
        if REORDER_FIRST:
            vorder = [(q, ci) for q in (0, 1) for ci in range(min(3, nchunk))] + \
                     [(q, ci) for q in (0, 1) for ci in range(3, nchunk)] + \
                     [(q, ci) for q in range(2, 8) for ci in range(nchunk)]
        else:
            vorder = [(q, ci) for q in range(8) for ci in range(nchunk)]
        vslot = {}
        for (q, ci) in vorder:
            if q not in vslot:
                vslot[q] = load_w(b_v_d[q], 4096)
            s = vslot[q]
            t0 = chunks[ci]
            if True:
                c0 = t0 - a + 2
                bk = nbank()
                if FINE_FIRST and (q, ci) == vorder[0]:
                    for kk in range(NCH):
                        S.op("pe", lambda e, kk=kk, bk=bk, c0=c0, s=s: e.matmul(
                            ps[bk][:, 0:256], hT[:, kk, c0:c0 + 128], slots[s][:, kk * 256:(kk + 1) * 256],
                            start=(kk == 0), stop=(kk == NCH - 1)),
                            reads=[("slot", s), ("h", 0, kk)], writes=[("ps", bk)])
                else:
                    mm_group(bk, 256, [(hT[:, kk, c0:c0 + 128], slots[s][:, kk * 256:(kk + 1) * 256]) for kk in range(NCH)],
                             reads=[("slot", s)] + hseg(0 if ci < 3 else 1))
                t = ntmp()
                S.op("act", lambda e, t=t, bk=bk: e.activation(out=tmps[t][:, 0:256], in_=ps[bk][:, 0:256], func=AF.Gelu_apprx_tanh),
                     reads=[("ps", bk)], writes=[("tmp", t)])
                S.op("act", lambda e, t=t, ci=ci, q=q: e.activation(out=tmps[t][:, 256:512], in_=tmps[t][:, 0:256],
                                                                     func=AF.Square, accum_out=ss[:, ci * 8 + q:ci * 8 + q + 1]),
                     reads=[("tmp", t)], writes=[("tmp", t), ("ss", ci)])
                S.op("dve", lambda e, t=t, ci=ci, q=q: e.tensor_copy(out=vv(ci, q * 256, (q + 1) * 256), in_=tmps[t][:, 0:256]),
                     reads=[("tmp", t)], writes=[("v", ci)])
        ssv = ss[:, 0:8 * nchunk].rearrange("p (c q) -> p c q", q=8)
        S.op("dve", lambda e: e.tensor_reduce(out=ssum[:, 0:nchunk], in_=ssv, axis=mybir.AxisListType.X, op=ALU.add),
             reads=[("ss", ci) for ci in range(nchunk)], writes=[("ssum",)])
        S.op("act", lambda e: e.activation(out=rv[:, 0:nchunk], in_=ssum[:, 0:nchunk], func=AF.Sqrt, scale=1.0 / D, bias=epsb[:, 0:1]),
             reads=[("ssum",), ("epsb",)], writes=[("rv",)])
        S.op("dve", lambda e: e.reciprocal(out=rv[:, 0:nchunk], in_=rv[:, 0:nchunk]), reads=[("rv",)], writes=[("rv",)])
        for ci in range(nchunk):
            S.op("dve", lambda e, ci=ci: e.tensor_scalar(out=wsr[:, ci, :], in0=wsTm[:, :], scalar1=rv[:, ci:ci + 1],
                                                         scalar2=None, op0=ALU.mult),
                 reads=[("rv",), ("wsTm",)], writes=[("wsr", ci)])
        for jg in range(8):
            s = load_w(b_u_d[jg], 4096)
            for ji in range(2):
                j = jg * 2 + ji
                h = jg
                for wi, cis in enumerate(bw):
                    if not cis:
                        continue
                    t0 = chunks[cis[0]]
                    nf = 128 * len(cis)
                    wt = tail if wi == 1 else 0
                    n = nf + wt
                    c0 = t0 - a + 2
                    bu = nbank()
                    mm_group(bu, n, [(slots[s][:, (ji * 16 + kk) * 128:(ji * 16 + kk + 1) * 128], hT[:, kk, c0:c0 + n])
                                     for kk in range(NCH)],
                             reads=[("slot", s)] + hseg(wi))
                    bg = nbank()

                    def gfn(e, cis=cis, bg=bg, j=j, h=h, wt=wt, nf=nf):
                        ins = None
                        for i, ci in enumerate(cis):
                            ins = e.matmul(ps[bg][:, i * 128:(i + 1) * 128], vv(ci, j * 128, (j + 1) * 128),
                                           wsr[:, ci, h * 128:(h + 1) * 128], start=True, stop=True)
                        if wt:
                            ci = nchunk - 1
                            ins = e.matmul(ps[bg][:, nf:nf + wt], vv(ci, j * 128, (j + 1) * 128),
                                           wsr[:, ci, h * 128:h * 128 + wt], start=True, stop=True)
                        return ins
                    gcis = list(cis) + ([nchunk - 1] if wt else [])
                    S.op("pe", gfn, reads=[("v", ci) for ci in gcis] + [("wsr", ci) for ci in gcis], writes=[("ps", bg)])
                    t1, t2 = ntmp(), ntmp()
                    S.op("act", lambda e, t1=t1, bu=bu, n=n: e.activation(out=tmps[t1][:, 0:n], in_=ps[bu][:, 0:n], func=AF.Gelu_apprx_tanh),
                         reads=[("ps", bu)], writes=[("tmp", t1)])
                    nci = len(cis)
                    S.op("dve", lambda e, t2=t2, bg=bg, nf=nf, j=j, h=h, nci=nci: e.scalar_tensor_tensor(
                        out=tmps[t2][:, 0:nf].rearrange("p (c t) -> p c t", t=128),
                        in0=ps[bg][:, 0:nf].rearrange("p (c t) -> p c t", t=128), scalar=cc(VN + j),
                        in1=cst[:, BSB + h * 128:BSB + (h + 1) * 128].unsqueeze(1).broadcast_to([128, nci, 128]),
                        op0=ALU.mult, op1=ALU.add),
                        reads=[("ps", bg), ("cst",)], writes=[("tmp", t2)])
                    if wt:
                        S.op("dve", lambda e, t2=t2, bg=bg, nf=nf, wt=wt, j=j, h=h: e.scalar_tensor_tensor(
                            out=tmps[t2][:, nf:nf + wt], in0=ps[bg][:, nf:nf + wt], scalar=cc(VN + j),
                            in1=cst[:, BSB + h * 128:BSB + h * 128 + wt], op0=ALU.mult, op1=ALU.add),
                            reads=[("ps", bg), ("cst",)], writes=[("tmp", t2)])
                    S.op("dve", lambda e, t1=t1, t2=t2, j=j, t0=t0, n=n: e.tensor_tensor(
                        out=ugv(j, t0, n), in0=tmps[t1][:, 0:n], in1=tmps[t2][:, 0:n], op=ALU.mult),
                        reads=[("tmp", t1), ("tmp", t2)], writes=[("ug", j, wi)])
        wins = [(chunks[cis[0]], 128 * len(cis) + (tail if wi == 1 else 0)) for wi, cis in enumerate(bw) if cis]
        proj_out(k, b_o_d, "ug", ugv, wins, NCH, cb=cb)

    XG = 8

    def load_x(k, c):
        if c % XG != XG - 1:
            return
        g0 = c - (XG - 1)
        a, b = ST[k]
        ln = b - a
        S.op("sp", lambda e: e.dma_start(out=x[:, g0:g0 + XG, 0:ln + 2], in_=xh[:, g0:g0 + XG, a:a + ln + 2]),
             reads=[], writes=[kk for cc_ in range(g0, g0 + XG) for kk in xk(cc_, 0, XW)], dma_sem=f"d_x{g0 // XG}")

    def b_geom(k):
        a, b = ST[k]
        first = a + 4 if k == 0 else a
        lo = first - a + 2
        return lo, lo + 384

    def stage_list(k):
        a, b = ST[k]
        ln = b - a
        out = []

        def conv_split(lo_local):
            w = conv_windows(k, lo_local)
            return w[1][0] - a + 2
        if "A" in stages:
            out.append(("A", 0, conv_split(a), ln + 2))
        if "F0" in stages:
            out.append(("F0", 2, conv_split(a), ln + 2))
        if "B" in stages:
            lo, sp = b_geom(k)
            out.append(("B", lo, sp, ln + 2))
        lo1 = max(a, OUT0)
        if "F1" in stages:
            out.append(("F1", OUT0 + 2 if k == 0 else 2, conv_split(lo1), ln + 2))
        out.append(("FIN", lo1 - a + 2, conv_split(lo1), ln + 2))
        return out

    GC = {"A": G_A, "F0": G_F0, "B": G_B, "F1": G_F1, "FIN": G_FIN}

    def make_prefetch(kn):
        an, bn = ST[kn]
        lnn = bn - an
        _, lo_n, sp_n, hi_n = stage_list(kn)[0]
        st = {}

        def load_chunk(cn):
            t = ntmp()
            S.op("sp", lambda e: e.dma_start(out=tmps[t][:, 0:lnn + 2], in_=xh[:, cn, an:an + lnn + 2]),
                 reads=[], writes=[("tmp", t)], dma_sem=f"d_o{t}")
            return t

        def step(c):
            if c == 0:
                st["ctx"] = norm_begin(lo_n, sp_n, hi_n)
            ctxn = st["ctx"]
            p1 = {0: (0, 1, 2), 1: (3, 4, 5), 2: (6, 7, 8), 3: (9, 10, 11), 4: (12, 13, 14), 5: (15,)}
            p2 = {6 + i: (2 * i, 2 * i + 1) for i in range(8)}
            if c in p1:
                for cn in p1[c]:
                    t = load_chunk(cn)
                    for pi, (c0, c1) in enumerate(ctxn["pieces"]):
                        r = rr["sqr"]
                        rr["sqr"] = (r + 1) % NSQ
                        bk = nbanks(ctxn)[pi]
                        S.op("act", lambda e, r=r, t=t, c0=c0, c1=c1: e.activation(
                            out=sqr[r][:, 0:c1 - c0], in_=tmps[t][:, c0:c1], func=AF.Square),
                            reads=[("tmp", t)], writes=[("sqr", r)])
                        S.op("pe", lambda e, r=r, c0=c0, c1=c1, bk=bk, cn=cn: e.matmul(
                            ps[bk][:, 0:c1 - c0], ones[:, :], sqr[r][:, 0:c1 - c0], start=(cn == 0), stop=(cn == NCH - 1)),
                            reads=[("sqr", r), ("ones",)], writes=[("ps", bk)])
            if c == 6:
                st["tr"] = norm_finish(ctxn, G_A, out_mode="final")
                pinned.add(st["tr"])
            if c in p2:
                tr = st["tr"]
                for cn in p2[c]:
                    t = load_chunk(cn)
                    for pi, (c0, c1) in enumerate(ctxn["pieces"]):
                        S.op("dve", lambda e, t=t, cn=cn, c0=c0, c1=c1, tr=tr: e.scalar_tensor_tensor(
                            out=hT[:, cn, c0:c1], in0=tmps[t][:, c0:c1], scalar=cc(G_A + cn), in1=tmps[tr][:, c0:c1],
                            op0=ALU.mult, op1=ALU.mult),
                            reads=[("tmp", t), ("tmp", tr), ("cst",)], writes=[("h", pi, cn)])
            if c == 13:
                pinned.discard(st["tr"])
        return step

    for c in range(NCH):
        load_x(0, c)
    pre_h = False
    pending = []
    for k in range(n_st):
        a, b = ST[k]
        ln = b - a
        sl = stage_list(k)
        skip_first_norm = pre_h
        pre_h = False
        interleave, pending = pending, []
        if not skip_first_norm:
            ctx = norm_begin(*sl[0][1:])
            for c in range(NCH):
                norm_chunk(ctx, c)
        for si, (name, lo, sp, hi) in enumerate(sl):
            if name == "FIN":
                break
            hook = None
            l = {"F0": 0, "F1": 1}.get(name)
            if l is not None:
                def hook(l=l, k=k):
                    S.op("dve", lambda e, l=l: e.tensor_copy(out=hT[:, :, 0:2], in_=carry[l][:, :, :]),
                         reads=[("carry", l)], writes=hseg(0))
                    if l == 1 and k == 0:
                        c0 = OUT0
                        S.op("dve", lambda e, c0=c0: e.memset(hT[:, :, c0:c0 + 2], 0.0), reads=[], writes=hseg(0))
            if not (si == 0 and skip_first_norm):
                norm_finish(ctx, GC[name], after_p0=hook)
            if l is not None and k + 1 < len(ST):
                S.op("dve", lambda e, l=l, ln=ln: e.tensor_copy(out=carry[l][:, :, :], in_=hT[:, :, ln:ln + 2]),
                     reads=hseg(1), writes=[("carry", l)])
            ctx = norm_begin(*sl[si + 1][1:])
            cb = (lambda c, ctx=ctx: norm_chunk(ctx, c))
            if PREFETCH_H and sl[si + 1][0] == "FIN" and k + 1 < n_st and stage_list(k + 1)[0][0] == "A":
                pf = make_prefetch(k + 1)
                cb = (lambda c, ctx=ctx, pf=pf: (norm_chunk(ctx, c), pf(c)))
                pre_h = True
            if name == "A":
                mixer_a(k, cb, interleave)
            elif name == "F0":
                ffn(k, 0, a, cb)
            elif name == "B":
                mixer_b(k, cb)
            else:
                ffn(k, 1, max(a, OUT0), cb)
        name, lo, sp, hi = sl[-1]
        tr = norm_finish(ctx, G_FIN, out_mode="final")
        pinned.add(tr)
        nout = hi - lo
        o0 = max(a, OUT0) - OUT0
        ops = []
        for c in range(NCH):
            def fin_op(c=c, lo=lo, hi=hi, nout=nout, tr=tr, o0=o0):
                t = ntmp()
                S.op("dve", lambda e: e.scalar_tensor_tensor(
                    out=tmps[t][:, 0:nout], in0=x[:, c, lo:hi], scalar=cc(G_FIN + c), in1=tmps[tr][:, lo:hi],
                    op0=ALU.mult, op1=ALU.mult),
                    reads=xk(c, lo, hi) + [("tmp", tr), ("cst",)], writes=[("tmp", t)])
                S.op("sp", lambda e: e.dma_start(out=out_d[:, c, o0:o0 + nout], in_=tmps[t][:, 0:nout]),
                     reads=[("tmp", t)], writes=[], dma_sem=f"d_o{t}")
            ops.append(fin_op)

        def tail_op(k=k, tr=tr):
            pinned.discard(tr)
            if k + 1 < n_st:
                for c in range(NCH):
                    load_x(k + 1, c)
        ops.append(tail_op)
        if pre_h:
            pending = ops
        else:
            for f in ops:
                f()
    assert not pending

    eng_attr = {"pe": "tensor", "act": "scalar", "dve": "vector", "pool": "gpsimd", "sp": "sync"}
    with nc.Block() as block:
        def make(engname):
            def body(e):
                for item in S.streams[engname]:
                    if item[0] == "wait":
                        e.wait_ge(sems[item[1]], item[2])
                    else:
                        ins = item[1](e)
                        ins.then_inc(sems[item[2]], item[3])
                if engname == "sp":
                    for i in range(NTMP):
                        if S.dma_cnt.get(f"d_o{i}", 0) > 0:
                            e.wait_ge(sems[f"d_o{i}"], S.dma_cnt[f"d_o{i}"])
            return body
        for en in ENGS:
            getattr(block, eng_attr[en])(make(en))
    es.close()
    return nc


def _slab(W, col0, ncols):
    K = W.shape[0]
    return W[:, col0:col0 + ncols].reshape(K // 128, 128, ncols).transpose(1, 0, 2)


def _fm(v):
    return v.reshape(-1, 128).T


def prepare_inputs(x, a_norm, a_in, a_conv, a_out, b_norm, b_in, b_vnorm, b_ws, b_bs, b_out,
                   f_norm, f_up, f_conv_w, f_conv_b, f_down, final_norm):
    f = np.float32
    shared = {}
    cst = np.zeros((128, NCST), f)
    cst[:, G_A:G_A + 16] = _fm(a_norm[0])
    cst[:, G_F0:G_F0 + 16] = _fm(f_norm[0])
    cst[:, G_B:G_B + 16] = _fm(b_norm[0])
    cst[:, G_F1:G_F1 + 16] = _fm(f_norm[1])
    cst[:, G_FIN:G_FIN + 16] = _fm(final_norm)
    cst[:, VN:VN + 16] = _fm(b_vnorm[0])
    cst[:, AC:AC + 48] = np.stack([_fm(a_conv[0, t]) for t in range(3)], axis=-1).reshape(128, 48)
    for l in range(2):
        cst[:, FW[l]:FW[l] + 264] = np.stack([_fm(f_conv_w[l, t]) for t in range(3)], axis=-1).reshape(128, 264)
        cst[:, FB[l]:FB[l] + 88] = _fm(f_conv_b[l])
    cst[:, BSB:BSB + 1024] = np.broadcast_to(b_bs[0].reshape(1, 1024), (128, 1024))
    shared["wst"] = np.ascontiguousarray(b_ws[0].transpose(2, 0, 1).reshape(128, 1024))
    s_idx = np.arange(128)[:, None, None]
    t_idx = np.arange(128)[None, None, :]
    shared["wmask"] = np.ascontiguousarray(np.broadcast_to((s_idx <= t_idx), (128, 8, 128)).astype(f).reshape(128, 1024))
    W = a_in[0]
    shared["a_in"] = np.ascontiguousarray(np.stack(
        [np.stack([_slab(W, part * D + j * 128, 128) for part in range(3)], axis=1).reshape(128, 6144) for j in range(16)]))
    def pairs16(Wm):
        return np.ascontiguousarray(np.stack(
            [np.stack([_slab(Wm, (cp * 2 + ci) * 128, 128) for ci in range(2)], axis=1).reshape(128, 4096) for cp in range(8)]))
    shared["a_out"] = pairs16(a_out[0])
    shared["b_u"] = pairs16(b_in[0][:, 0:D])
    shared["b_o"] = pairs16(b_out[0])
    shared["b_v"] = np.ascontiguousarray(np.stack([_slab(b_in[0], D + q * 256, 256).reshape(128, 4096) for q in range(8)]))
    shared["f_up"] = np.ascontiguousarray(np.stack(
        [np.stack([_slab(f_up[l], part * DFF + j * 128, 128) for part in range(2)], axis=1).reshape(128, 4096)
         for l in range(2) for j in range(NFC)]))
    fd = []
    for l in range(2):
        for g in range(NG):
            Wg = f_down[l][g * GJ * 128:(g + 1) * GJ * 128]
            for cp in range(8):
                fd.append(np.stack([_slab(Wg, (cp * 2 + ci) * 128, 128) for ci in range(2)], axis=1).reshape(128, 2 * GJ * 128))
    shared["f_dn"] = np.ascontiguousarray(np.stack(fd))
    in_maps = []
    for core in range(NCORES):
        bi, qi = core // 4, core % 4
        Sq = x.shape[1]
        xp = np.zeros((L + 2, D), f)
        lo = qi * 2048 - 6
        src_lo, src_hi = max(lo, 0), min(lo + L + 2, Sq)
        xp[src_lo - lo:src_hi - lo] = x[bi, src_lo:src_hi]
        m = dict(shared)
        m["xh"] = np.ascontiguousarray(xp.reshape(L + 2, NCH, 128).transpose(2, 1, 0))
        m["cst"] = cst
        in_maps.append(m)
    return in_maps


_NC_CACHE = {}


def kernel(**inputs):
    inputs = {k: np.asarray(v, dtype=np.float32) for k, v in inputs.items()}
    in_maps = prepare_inputs(**inputs)
    if "nc" not in _NC_CACHE:
        _NC_CACHE["nc"] = build_program()
    nc = _NC_CACHE["nc"]
    res = run_bass_kernel_spmd(nc, in_maps, core_ids=list(range(NCORES)))
    B, Sq = inputs["x"].shape[0], inputs["x"].shape[1]
    out = np.empty((B, Sq, D), np.float32)
    for core in range(NCORES):
        bi, qi = core // 4, core % 4
        o = np.asarray(res.results[core]["out"]).transpose(2, 1, 0).reshape(TOK, D)
        t0 = 0 if qi == 0 else 2
        g0, g1 = qi * 2048 + t0, min(qi * 2048 + TOK, Sq)
        out[bi, g0:g1, :] = o[t0:t0 + (g1 - g0)]
    return out
```
